# Optimizing a Trainium2 kernel written in Bass

```python
import jax, jax.numpy as jnp
from jax import lax
import numpy as np

D_MODEL = 1024
BATCH = 8
SEQ = 2048
DEPTH = 2
DEC_BATCH = 128
DEC_SEQ = 4
PAST_LEN = 16384
PAGE_SIZE = 128

N_MIXERS = 2
N_LRU_LAYERS = (DEPTH + 1) // 2
N_GM_LAYERS = DEPTH // 2
LRU_WIDTH = D_MODEL
LRU_HEADS = 4
LRU_HEAD_DIM = LRU_WIDTH // LRU_HEADS
CONV_W = 4
LRU_C = 8.0
GM_WIDTH = D_MODEL
GM_GROUPS = 8
GM_GROUP_DIM = GM_WIDTH // GM_GROUPS
CHUNK = 128
D_FF = 3 * D_MODEL
FFN_CONV_W = 3
PLE_DIM = 256
EPS = 1e-6

kernel_name = "hybrid_rglru_gmlp_convffn_decode_step"


def rmsnorm(x, g):
    xf = x.astype(jnp.float32)
    y = xf * lax.rsqrt(jnp.mean(xf * xf, axis=-1, keepdims=True) + EPS)
    return (y * g.astype(jnp.float32)).astype(x.dtype)


def layernorm(x, g, b):
    xf = x.astype(jnp.float32)
    mu = jnp.mean(xf, axis=-1, keepdims=True)
    xc = xf - mu
    y = xc * lax.rsqrt(jnp.mean(xc * xc, axis=-1, keepdims=True) + EPS)
    return (y * g.astype(jnp.float32) + b.astype(jnp.float32)).astype(x.dtype)


def causal_dwconv(x, past, w, b):
    k_w = w.shape[0]
    t = x.shape[1]
    xp = jnp.concatenate([past.astype(x.dtype), x], axis=1)
    y = xp[:, 0:t] * w[0]
    for k in range(1, k_w):
        y = y + xp[:, k:k + t] * w[k]
    return y + b, xp[:, xp.shape[1] - (k_w - 1):]


def block_diag_linear(x, w, b):
    bsz, t, _ = x.shape
    xh = x.reshape(bsz, t, LRU_HEADS, LRU_HEAD_DIM)
    return jnp.einsum('bthi,hij->bthj', xh, w).reshape(bsz, t, LRU_WIDTH) + b


def rg_lru(xc, h0, w_a, b_a, w_x, b_x, lam):
    xf = xc.astype(jnp.float32)
    r = jax.nn.sigmoid(block_diag_linear(xc, w_a, b_a).astype(jnp.float32))
    i = jax.nn.sigmoid(block_diag_linear(xc, w_x, b_x).astype(jnp.float32))
    log_a = -LRU_C * r * jax.nn.softplus(-lam.astype(jnp.float32))
    a = jnp.exp(log_a)
    mult = jnp.sqrt(-jnp.expm1(2.0 * log_a))
    bterm = mult * (i * xf)
    bterm = bterm.at[:, 0].add(a[:, 0] * h0.astype(jnp.float32))

    def combine(c1, c2):
        a1, b1 = c1
        a2, b2 = c2
        return a1 * a2, a2 * b1 + b2

    _, h = lax.associative_scan(combine, (a, bterm), axis=1)
    return h.astype(xc.dtype), h[:, -1].astype(xc.dtype)


def lru_mixer(xn, h0, conv_past, w_in, conv_w, conv_b, w_a, b_a, w_x, b_x, lam, w_out):
    proj = xn @ w_in
    gate, xb = jnp.split(proj, 2, axis=-1)
    xc, conv_tail = causal_dwconv(xb, conv_past, conv_w, conv_b)
    h, h_last = rg_lru(xc, h0, w_a, b_a, w_x, b_x, lam)
    y = (jax.nn.gelu(gate) * h) @ w_out
    return y, h_last, conv_tail


def gmlp_mixer(xn, w_in, ln_g, ln_b, w_s, b_s, w_out):
    z = jax.nn.gelu(xn @ w_in)
    u, v = jnp.split(z, 2, axis=-1)
    v = layernorm(v, ln_g, ln_b)
    bsz, t, _ = v.shape
    n_chunks = -(-t // CHUNK)
    pad = n_chunks * CHUNK - t
    vp = jnp.pad(v, ((0, 0), (0, pad), (0, 0))).reshape(bsz, n_chunks, CHUNK, GM_GROUPS, GM_GROUP_DIM)
    mask = jnp.tril(jnp.ones((CHUNK, CHUNK), dtype=bool))
    ws = jnp.where(mask[None], w_s, jnp.zeros_like(w_s))
    s = jnp.einsum('gts,bcsgd->bctgd', ws, vp) + jnp.transpose(b_s)[None, None, :, :, None]
    s = s.reshape(bsz, n_chunks * CHUNK, GM_WIDTH)[:, :t]
    y = (u * s) @ w_out
    return y, v


def conv_ffn(xn, past, w_up, conv_w, conv_b, w_down):
    up = xn @ w_up
    g, u = jnp.split(up, 2, axis=-1)
    gc, tail = causal_dwconv(g, past, conv_w, conv_b)
    return (jax.nn.gelu(gc) * u) @ w_down, tail


def _trunk(x, p, lru_h_past, lru_conv_past, ffn_conv_past, weights):
    (norm_mix, norm_ffn, norm_ple, norm_final,
     lru_w_in, lru_conv_w, lru_conv_b, lru_w_a, lru_b_a, lru_w_x, lru_b_x, lru_lambda, lru_w_out,
     gm_w_in, gm_ln_g, gm_ln_b, gm_w_s, gm_b_s, gm_w_out,
     ffn_w_up, ffn_conv_w, ffn_conv_b, ffn_w_down,
     ple_w_gate, ple_w_proj) = weights
    h = x
    lru_h_new, lru_conv_new, ffn_conv_new, gm_v_new = [], [], [], []
    for i in range(DEPTH):
        j = i // N_MIXERS
        xn = rmsnorm(h, norm_mix[i])
        if i % N_MIXERS == 0:
            y, h_last, tail = lru_mixer(xn, lru_h_past[j], lru_conv_past[j], lru_w_in[j],
                                        lru_conv_w[j], lru_conv_b[j], lru_w_a[j], lru_b_a[j],
                                        lru_w_x[j], lru_b_x[j], lru_lambda[j], lru_w_out[j])
            lru_h_new.append(h_last)
            lru_conv_new.append(tail)
        else:
            y, v = gmlp_mixer(xn, gm_w_in[j], gm_ln_g[j], gm_ln_b[j], gm_w_s[j], gm_b_s[j], gm_w_out[j])
            gm_v_new.append(v)
        h = h + y
        y, tail = conv_ffn(rmsnorm(h, norm_ffn[i]), ffn_conv_past[i], ffn_w_up[i],
                           ffn_conv_w[i], ffn_conv_b[i], ffn_w_down[i])
        ffn_conv_new.append(tail)
        h = h + y
        gate = jax.nn.sigmoid(rmsnorm(h, norm_ple[i]) @ ple_w_gate[i])
        h = h + gate * (p[i] @ ple_w_proj[i])
    out = rmsnorm(h, norm_final)
    return out, jnp.stack(lru_h_new), jnp.stack(lru_conv_new), jnp.stack(ffn_conv_new), gm_v_new


def setup_inputs(seed: int = 0) -> dict:
    key = jax.random.key(seed)
    ks = iter(jax.random.split(key, 40))
    f32 = jnp.float32

    def nrm(shape, scale):
        return jax.random.normal(next(ks), shape, f32) * scale

    def gain(shape):
        return 1.0 + nrm(shape, 0.02)

    a0 = jax.random.uniform(next(ks), (N_LRU_LAYERS, LRU_WIDTH), f32, 0.9, 0.999)
    return {
        "x_prompt": nrm((BATCH, SEQ, D_MODEL), 1.0),
        "x_sample": nrm((DEC_BATCH, DEC_SEQ, D_MODEL), 1.0),
        "p_prompt": nrm((DEPTH, BATCH, SEQ, PLE_DIM), 1.0),
        "p_sample": nrm((DEPTH, DEC_BATCH, DEC_SEQ, PLE_DIM), 1.0),
        "state_lru_h": nrm((N_LRU_LAYERS, DEC_BATCH, LRU_WIDTH), 0.5),
        "state_lru_conv": nrm((N_LRU_LAYERS, DEC_BATCH, CONV_W - 1, LRU_WIDTH), 1.0),
        "state_ffn_conv": nrm((DEPTH, DEC_BATCH, FFN_CONV_W - 1, D_FF), 1.0),
        "norm_mix": gain((DEPTH, D_MODEL)),
        "norm_ffn": gain((DEPTH, D_MODEL)),
        "norm_ple": gain((DEPTH, D_MODEL)),
        "norm_final": gain((D_MODEL,)),
        "lru_w_in": nrm((N_LRU_LAYERS, D_MODEL, 2 * LRU_WIDTH), D_MODEL ** -0.5),
        "lru_conv_w": nrm((N_LRU_LAYERS, CONV_W, LRU_WIDTH), CONV_W ** -0.5),
        "lru_conv_b": nrm((N_LRU_LAYERS, LRU_WIDTH), 0.01),
        "lru_w_a": nrm((N_LRU_LAYERS, LRU_HEADS, LRU_HEAD_DIM, LRU_HEAD_DIM), LRU_HEAD_DIM ** -0.5),
        "lru_b_a": nrm((N_LRU_LAYERS, LRU_WIDTH), 0.01),
        "lru_w_x": nrm((N_LRU_LAYERS, LRU_HEADS, LRU_HEAD_DIM, LRU_HEAD_DIM), LRU_HEAD_DIM ** -0.5),
        "lru_b_x": nrm((N_LRU_LAYERS, LRU_WIDTH), 0.01),
        "lru_lambda": jnp.log(a0) - jnp.log1p(-a0),
        "lru_w_out": nrm((N_LRU_LAYERS, LRU_WIDTH, D_MODEL), LRU_WIDTH ** -0.5),
        "gm_w_in": nrm((N_GM_LAYERS, D_MODEL, 2 * GM_WIDTH), D_MODEL ** -0.5),
        "gm_ln_g": gain((N_GM_LAYERS, GM_WIDTH)),
        "gm_ln_b": nrm((N_GM_LAYERS, GM_WIDTH), 0.01),
        "gm_w_s": nrm((N_GM_LAYERS, GM_GROUPS, CHUNK, CHUNK), CHUNK ** -0.5),
        "gm_b_s": 1.0 + nrm((N_GM_LAYERS, GM_GROUPS, CHUNK), 0.01),
        "gm_w_out": nrm((N_GM_LAYERS, GM_WIDTH, D_MODEL), GM_WIDTH ** -0.5),
        "ffn_w_up": nrm((DEPTH, D_MODEL, 2 * D_FF), D_MODEL ** -0.5),
        "ffn_conv_w": nrm((DEPTH, FFN_CONV_W, D_FF), FFN_CONV_W ** -0.5),
        "ffn_conv_b": nrm((DEPTH, D_FF), 0.01),
        "ffn_w_down": nrm((DEPTH, D_FF, D_MODEL), D_FF ** -0.5),
        "ple_w_gate": nrm((DEPTH, D_MODEL, D_MODEL), D_MODEL ** -0.5),
        "ple_w_proj": nrm((DEPTH, PLE_DIM, D_MODEL), PLE_DIM ** -0.5),
    }


def reference(x_prompt, x_sample, p_prompt, p_sample, state_lru_h, state_lru_conv, state_ffn_conv,
              norm_mix, norm_ffn, norm_ple, norm_final,
              lru_w_in, lru_conv_w, lru_conv_b, lru_w_a, lru_b_a, lru_w_x, lru_b_x, lru_lambda, lru_w_out,
              gm_w_in, gm_ln_g, gm_ln_b, gm_w_s, gm_b_s, gm_w_out,
              ffn_w_up, ffn_conv_w, ffn_conv_b, ffn_w_down,
              ple_w_gate, ple_w_proj):
    weights = (norm_mix, norm_ffn, norm_ple, norm_final,
               lru_w_in, lru_conv_w, lru_conv_b, lru_w_a, lru_b_a, lru_w_x, lru_b_x, lru_lambda, lru_w_out,
               gm_w_in, gm_ln_g, gm_ln_b, gm_w_s, gm_b_s, gm_w_out,
               ffn_w_up, ffn_conv_w, ffn_conv_b, ffn_w_down,
               ple_w_gate, ple_w_proj)
    dt = x_prompt.dtype
    h0_p = jnp.zeros((N_LRU_LAYERS, BATCH, LRU_WIDTH), dt)
    conv0_p = jnp.zeros((N_LRU_LAYERS, BATCH, CONV_W - 1, LRU_WIDTH), dt)
    ffn0_p = jnp.zeros((DEPTH, BATCH, FFN_CONV_W - 1, D_FF), dt)
    y_prompt, new_lru_h_prompt, new_lru_conv_prompt, new_ffn_conv_prompt, _ = _trunk(
        x_prompt, p_prompt, h0_p, conv0_p, ffn0_p, weights)
    y_sample, new_lru_h_sample, new_lru_conv_sample, new_ffn_conv_sample, gm_v_s = _trunk(
        x_sample, p_sample, state_lru_h, state_lru_conv, state_ffn_conv, weights)
    new_gm_v_sample = jnp.stack(gm_v_s)
    return (y_prompt, y_sample, new_lru_h_prompt, new_lru_conv_prompt, new_ffn_conv_prompt,
            new_lru_h_sample, new_lru_conv_sample, new_ffn_conv_sample, new_gm_v_sample)
```

```python
import numpy as np
from contextlib import ExitStack
import concourse.bass as bass
import concourse.mybir as mybir
from concourse.bass_utils import run_bass_kernel_spmd

F32 = mybir.dt.float32
BF16 = mybir.dt.bfloat16
AF = mybir.ActivationFunctionType
ALU = mybir.AluOpType

ENGS = ("pe", "act", "dve", "pool", "sp")
PAGE = 512
SB_BYTES = 212480
EPS = 1e-6


class Buf:
    __slots__ = ("name", "w", "rs", "dsem", "dcnt")

    def __init__(self, name):
        self.name = name
        self.w = None
        self.rs = {}
        self.dsem = None
        self.dcnt = 0


class Op:
    __slots__ = ("eng", "idx", "fn", "deps", "is_dma", "signal", "ticket", "dbuf", "dval")

    def __init__(self, eng, idx, fn, is_dma):
        self.eng = eng
        self.idx = idx
        self.fn = fn
        self.deps = None
        self.is_dma = is_dma
        self.signal = False
        self.ticket = None
        self.dbuf = None
        self.dval = None


class Prog:
    def __init__(self, nc, same_eng_dist=3):
        self.nc = nc
        self.ops = {e: [] for e in ENGS}
        self.same_eng_dist = same_eng_dist
        self.dma_bufs = []

    def add(self, eng, fn, reads=(), writes=(), dma=False, key=None):
        lst = self.ops[eng]
        op = Op(eng, len(lst), fn, dma)
        best = {}
        ddeps = {}

        def consider(d):
            if d is None or d is op:
                return
            if d.is_dma:
                ddeps[id(d.dbuf)] = (d.dbuf, 16 * d.dbuf.dcnt)
                return
            if d.eng == eng and not dma:
                if eng == "pe":
                    return
                if op.idx - d.idx > self.same_eng_dist:
                    return
            cur = best.get(d.eng)
            if cur is None or cur.idx < d.idx:
                best[d.eng] = d

        for b in reads:
            consider(b.w)
        for b in writes:
            consider(b.w)
            for r in b.rs.values():
                consider(r)
        op.deps = list(best.values()) + list(ddeps.values())
        if dma:
            if key is None:
                key = (list(writes) + list(reads))[0]
            if key.dsem is None:
                key.dsem = -1
                self.dma_bufs.append(key)
            key.dcnt += 1
            op.dbuf = key
            op.dval = 16 * key.dcnt
        for b in reads:
            if dma:
                b.rs[("dma", id(op))] = op
            else:
                b.rs[eng] = op
        for b in writes:
            b.w = op
            b.rs = {}
        lst.append(op)
        return op

    def finalize(self, final_wait_bufs=()):
        nc = self.nc
        for e in ENGS:
            for op in self.ops[e]:
                for d in op.deps:
                    if not isinstance(d, tuple):
                        d.signal = True
        for e in ENGS:
            t = 0
            for op in self.ops[e]:
                if op.signal and not op.is_dma:
                    t += 1
                    op.ticket = t
        with ExitStack() as st:
            esem = {e: st.enter_context(nc.semaphore("s_" + e)) for e in ENGS}
            for i, b in enumerate(self.dma_bufs):
                b.dsem = st.enter_context(nc.semaphore("d%d" % i))
            block = st.enter_context(nc.Block())
            engobj = {"pe": block.tensor, "act": block.scalar, "dve": block.vector,
                      "pool": block.gpsimd, "sp": block.sync}
            prog = self

            def make(e):
                def body(eng):
                    seen = {}
                    for op in prog.ops[e]:
                        need = {}
                        for d in op.deps:
                            if isinstance(d, tuple):
                                k, v = d[0].dsem, d[1]
                            else:
                                k, v = esem[d.eng], d.ticket
                            kk = id(k)
                            if kk not in need or need[kk][1] < v:
                                need[kk] = (k, v)
                        for kk, (k, v) in need.items():
                            if seen.get(kk, 0) >= v:
                                continue
                            seen[kk] = v
                            eng.wait_ge(k, v)
                        ins = op.fn(eng)
                        if op.is_dma:
                            ins.then_inc(op.dbuf.dsem, 16)
                        elif op.signal:
                            ins.then_inc(esem[e], 1)
                    if e == "sp":
                        done = set()
                        for b in final_wait_bufs:
                            if b.dsem is not None and b.dcnt > 0 and id(b) not in done:
                                done.add(id(b))
                                eng.wait_ge(b.dsem, 16 * b.dcnt)
                return body

            for e in ENGS:
                engobj[e](make(e))


class V:
    __slots__ = ("ap", "bufs")

    def __init__(self, ap, bufs):
        self.ap = ap
        self.bufs = bufs


class Mem:
    def __init__(self, SB):
        self.SB = SB
        self.pages = [Buf("pg%d" % i) for i in range(SB_BYTES // PAGE)]
        self.top = 0

    def alloc(self, nbytes, align=PAGE):
        off = (self.top + align - 1) // align * align
        self.top = off + nbytes
        assert self.top <= SB_BYTES, ("SBUF overflow", self.top)
        return off

    def pg(self, off, nbytes):
        return self.pages[off // PAGE:(off + nbytes - 1) // PAGE + 1]


class T:
    def __init__(self, mem, shape, dtype, off=None, align=PAGE, parts=128):
        self.mem = mem
        self.dtype = dtype
        self.shape = tuple(shape)
        self.esz = 4 if dtype == F32 else 2
        n = 1
        for s in self.shape:
            n *= s
        nb = (n * self.esz + 3) // 4 * 4
        if off is None:
            off = mem.alloc(nb, align)
        assert off % 4 == 0
        self.off = off
        self.nbytes = nb
        base = mem.SB[0:parts, off // 4:off // 4 + nb // 4]
        if dtype != F32:
            base = base.bitcast(dtype)
            if base.shape[1] != n:
                base = base[:, 0:n]
        if len(self.shape) == 2:
            base = base.rearrange("p (a b) -> p a b", a=self.shape[0])
        elif len(self.shape) == 3:
            base = base.rearrange("p (a b c) -> p a b c", a=self.shape[0], b=self.shape[1])
        elif len(self.shape) == 4:
            base = base.rearrange("p (a b c d) -> p a b c d", a=self.shape[0], b=self.shape[1], c=self.shape[2])
        self.ap = base
        self.bufs = mem.pg(off, nb)
        st = []
        acc = 1
        for s in reversed(self.shape):
            st.append(acc)
            acc *= s
        self.strides = tuple(reversed(st))

    def v(self, *idx, p=None):
        key = [slice(None) if p is None else slice(p[0], p[1])]
        lo = 0
        hi = 0
        for d, s in enumerate(self.shape):
            i = idx[d] if d < len(idx) else None
            if i is None:
                key.append(slice(None))
                hi += (s - 1) * self.strides[d]
            elif isinstance(i, tuple):
                key.append(slice(i[0], i[1]))
                lo += i[0] * self.strides[d]
                hi += (i[1] - 1) * self.strides[d]
            else:
                key.append(i)
                lo += i * self.strides[d]
                hi += i * self.strides[d]
        ap = self.ap[tuple(key)]
        return V(ap, self.mem.pg(self.off + lo * self.esz, (hi - lo + 1) * self.esz))

    def all(self):
        return V(self.ap, self.bufs)


def bufs_of(vs):
    out = []
    seen = set()
    for v in vs:
        for b in v.bufs:
            if id(b) not in seen:
                seen.add(id(b))
                out.append(b)
    return out


D = 1024
KC = 8
SEQ = 2048
NS = 64
NSEQ = 16
NTOK = SEQ + NS
DFF = 3072
SL = 512
NSL = DFF // SL
HROW = 1152

VC = {}
_o = 0
for _n, _w in [("nm0", 8), ("nm1", 8), ("nf0", 8), ("nf1", 8), ("np0", 8), ("np1", 8), ("nfin", 8),
               ("lcw", 32), ("lcb", 8), ("ba", 8), ("bx", 8), ("lam", 8),
               ("fcw0", 72), ("fcw1", 72), ("fcb0", 24), ("fcb1", 24)]:
    VC[_n] = _o
    _o += _w
NV = _o

DBG_GROUP = 0
POOLENG = "pool"
GROUPS = [
    dict(p0=0, np=1024, sample=False),
    dict(p0=1024, np=1024, sample=True),
]


def build_program(stop_after=None, debug=False):
    nc = bass.Bass("TRN2", target_bir_lowering=False)

    def din(name, shape):
        return nc.dram_tensor(name, list(shape), F32, kind="ExternalInput").ap()

    def dout(name, shape):
        return nc.dram_tensor(name, list(shape), F32, kind="ExternalOutput").ap()

    xT = din("xT", [D, NTOK])
    pT = din("pT", [2, 256, NTOK])
    st_h0 = din("st_h0", [D, NSEQ])
    st_lc = din("st_lc", [D, NSEQ * 3])
    st_fc = din("st_fc", [2, DFF, NSEQ * 2])
    vecs = din("vecs", [128, NV])
    cst = din("cst", [128, 256])
    m64 = din("m64", [64, 64])
    bdraw = din("bdraw", [64, 512])
    bsrow = din("bsrow", [1, 1024])
    lng_d = din("lng_bc", [128, D])
    lnb_d = din("lnb_bc", [128, D])
    wsT_d = din("wsT", [128, 1024])
    w_lin = din("lru_w_in", [D, 2048])
    w_la = din("lru_w_a", [D, 256])
    w_lx = din("lru_w_x", [D, 256])
    w_lout = din("lru_w_out", [D, D])
    w_gin = din("gm_w_in", [D, 2048])
    w_gout = din("gm_w_out", [D, D])
    w_up = din("ffn_w_up", [2, D, 2 * DFF])
    w_dn = din("ffn_w_down", [2, DFF, D])
    w_pg = din("ple_w_gate", [2, D, D])
    w_pp = din("ple_w_proj", [2, 256, D])

    yT = dout("yT", [D, NTOK])
    o_h = dout("o_h", [D, 17])
    o_lc = dout("o_lc", [D, 51])
    o_fc = dout("o_fc", [2, DFF, 34])
    o_v = dout("o_v", [NS, D])
    dbg = dout("dbg", [6, D, 1088]) if debug else None

    with ExitStack() as st:
        SB = st.enter_context(nc.sbuf_tensor("SB", [128, SB_BYTES // 4], F32))
        PS = st.enter_context(nc.psum_tensor("PS", [128, 8, 512], F32))
        P = Prog(nc)
        mem = Mem(SB)
        psb = [Buf("ps%d" % i) for i in range(8)]
        ps_rr = [0]

        def psum(n=512):
            b = ps_rr[0]
            ps_rr[0] = (b + 1) % 8
            return V(PS[:, b, 0:n], [psb[b]])

        def psum2():
            b = ps_rr[0]
            if b % 2:
                b = (b + 1) % 8
            ps_rr[0] = (b + 2) % 8
            return [V(PS[:, b, :], [psb[b]]), V(PS[:, b + 1, :], [psb[b + 1]])]

        def emit(eng, fn, R=(), W=(), dma=False, key=None):
            return P.add(eng, fn, reads=bufs_of(R), writes=bufs_of(W), dma=dma, key=key)

        def act(out, in_, func, bias=None, scale=None, R=(), eng="act"):
            kw = {}
            rr = [in_] + list(R)
            if bias is not None:
                kw["bias"] = bias.ap if isinstance(bias, V) else bias
                if isinstance(bias, V):
                    rr.append(bias)
            if scale is not None:
                kw["scale"] = scale.ap if isinstance(scale, V) else scale
                if isinstance(scale, V):
                    rr.append(scale)
            emit("act", lambda e: e.activation(out=out.ap, in_=in_.ap, func=func, **kw), R=rr, W=[out])

        def tt(out, a, b, op, eng="dve"):
            emit(eng, lambda e: e.tensor_tensor(out=out.ap, in0=a.ap, in1=b.ap, op=op), R=[a, b], W=[out])

        def ts(out, a, s1, op0, s2=None, op1=None, eng="dve"):
            rr = [a]
            s1a = s1
            s2a = s2
            if isinstance(s1, V):
                rr.append(s1)
                s1a = s1.ap
            if isinstance(s2, V):
                rr.append(s2)
                s2a = s2.ap
            if op1 is None:
                emit(eng, lambda e: e.tensor_scalar(out=out.ap, in0=a.ap, scalar1=s1a, scalar2=None, op0=op0), R=rr, W=[out])
            else:
                emit(eng, lambda e: e.tensor_scalar(out=out.ap, in0=a.ap, scalar1=s1a, scalar2=s2a, op0=op0, op1=op1), R=rr, W=[out])

        def stt(out, a, s, b, op0, op1):
            rr = [a, b]
            sa = s
            if isinstance(s, V):
                rr.append(s)
                sa = s.ap
            emit("dve", lambda e: e.scalar_tensor_tensor(out=out.ap, in0=a.ap, scalar=sa, in1=b.ap, op0=op0, op1=op1), R=rr, W=[out])

        def cp(out, in_, eng="dve"):
            if eng == "act":
                emit("act", lambda e: e.activation(out=out.ap, in_=in_.ap, func=AF.Copy), R=[in_], W=[out])
            else:
                emit(eng, lambda e: e.tensor_copy(out=out.ap, in_=in_.ap), R=[in_], W=[out])

        def mset(out, val, eng="dve"):
            emit(eng, lambda e: e.memset(out.ap, val), W=[out])

        def mm(ps, lhsT, rhs, start, stop, R=()):
            emit("pe", lambda e: e.matmul(ps.ap, lhsT=lhsT.ap, rhs=rhs.ap, start=start, stop=stop),
                 R=[lhsT, rhs] + list(R), W=[ps])

        def dma(eng, out, in_, sb_side=None, key=None):
            if isinstance(out, V):
                emit(eng, lambda e: e.dma_start(out=out.ap, in_=in_), W=[out], dma=True, key=key)
            else:
                emit(eng, lambda e: e.dma_start(out=out, in_=in_.ap), R=[in_], dma=True, key=key)

        Hreg = T(mem, [KC, HROW], F32)
        Preg = T(mem, [2, HROW], BF16)
        R1a = T(mem, [KC, 2048], BF16)
        R1b_off = mem.alloc(24576)
        slot_off = [mem.alloc(24576), mem.alloc(24576)]
        VEC = T(mem, [NV], F32)
        DER = T(mem, [6, 8], F32, align=64)
        onesD = T(mem, [128], BF16, align=64)
        ones1 = T(mem, [128], BF16, align=64)
        ident = T(mem, [128], BF16, align=64)
        epsT = T(mem, [1], F32, align=64)
        cst_f = T(mem, [256], F32, align=64)
        carry = T(mem, [8], F32)
        LNS = T(mem, [12], F32, align=64)
        LNM = T(mem, [4], F32, align=64)
        xh_store = T(mem, [8, 4], BF16)
        fh_store = T(mem, [2, 24, 2], BF16, align=64)
        h0s = T(mem, [8, NSEQ], F32)
        lcs = T(mem, [8, NSEQ, 3], F32, align=64)
        fcs = T(mem, [24, NSEQ, 2], F32)
        OH = T(mem, [8, 17], F32)
        OLC = T(mem, [8, 17, 3], F32, align=64)
        OFC = T(mem, [24, 17, 2], F32)
        arena0 = (mem.top + PAGE - 1) // PAGE * PAGE
        print("fixed bytes", arena0, "arena", SB_BYTES - arena0)

        def arena_reset():
            mem.top = arena0

        dbg2 = {}

        def dump2(name, v, shape, dtype, gi):
            if debug and gi == DBG_GROUP and name not in dbg2:
                t = nc.dram_tensor("dbg_" + name, list(shape), F32, kind="ExternalOutput").ap()
                dbg2[name] = t
                dma("pool", t, v)
                final_bufs.extend(v.bufs)

        def dump(idx, gi_want, gi, ng):
            if debug and gi == gi_want:
                hvr = Hreg.v(None, (0, ng))
                dma("sp", dbg[idx].rearrange("(c p) n -> p c n", p=128)[:, :, 0:ng], hvr, key=KEY["H"])
                final_bufs.append(KEY["H"])

        final_bufs = []
        dma("sp", VEC.all(), vecs, VEC.all())
        dma("sp", cst_f.all(), cst, cst_f.all())
        cp(ident.all(), cst_f.v((0, 128)))
        mset(onesD.all(), 1.0 / 1024.0)
        mset(ones1.all(), 1.0)
        mset(epsT.all(), EPS)
        mset(carry.all(), 0.0)
        mset(xh_store.all(), 0.0)
        mset(fh_store.all(), 0.0)
        lam = VEC.v((VC["lam"], VC["lam"] + 8))
        act(DER.v(4), lam, AF.Exp, scale=-1.0)
        act(DER.v(5), DER.v(4), AF.Ln, bias=1.0)
        ts(DER.v(0), DER.v(5), -4.0, ALU.mult)
        ts(DER.v(1), DER.v(5), -8.0, ALU.mult)
        ts(DER.v(2), VEC.v((VC["ba"], VC["ba"] + 8)), 0.5, ALU.mult)
        ts(DER.v(3), VEC.v((VC["bx"], VC["bx"] + 8)), 0.5, ALU.mult)

        def vcol(name, c):
            return VEC.v((VC[name] + c, VC[name] + c + 1))

        def dcol(r, c):
            return DER.v(r, (c, c + 1))

        KEY = {n: Buf("key_" + n) for n in ("R1a", "R1ag", "R1b", "R1bo", "slot0", "slot1", "P", "H")}

        def load_w(dst_T, kidx, src_rows_ap, key):
            dv = dst_T.v(kidx)
            dma("pool", dv, src_rows_ap, key=KEY[key])

        def rows(w2d, k):
            return w2d[k * 128:(k + 1) * 128, :]

        def load_mat(dst_T, src2d, key):
            dma("pool", dst_T.all(), src2d.rearrange("(k p) n -> p k n", p=128), key=KEY[key])

        R1b_wa = T(mem, [KC, 256], BF16, off=R1b_off)
        R1b_wx = T(mem, [KC, 256], BF16, off=R1b_off + 4096)
        R1b_wout = T(mem, [KC, D], BF16, off=R1b_off + 8192)
        XNall = T(mem, [KC, 1088], BF16, off=R1b_off)

        def load_lru_weights():
            dma("pool", R1a.v(None, (1024, 2048)), w_lin[:, 1024:2048].rearrange("(k p) n -> p k n", p=128), key=KEY["R1a"])
            load_mat(R1b_wa, w_la, "R1b")
            load_mat(R1b_wx, w_lx, "R1b")
            dma("pool", R1a.v(None, (0, 1024)), w_lin[:, 0:1024].rearrange("(k p) n -> p k n", p=128), key=KEY["R1ag"])
            load_mat(R1b_wout, w_lout, "R1bo")

        def load_mixer0():
            dma("pool", R1a.v(None, (1024, 2048)), w_lin[:, 1024:2048].rearrange("(k p) n -> p k n", p=128), key=KEY["R1a"])

        def load_mixer0_rest_a():
            load_mat(R1b_wa, w_la, "R1b")
            load_mat(R1b_wx, w_lx, "R1b")

        def load_mixer0_rest_b():
            dma("pool", R1a.v(None, (0, 1024)), w_lin[:, 0:1024].rearrange("(k p) n -> p k n", p=128), key=KEY["R1ag"])
            load_mat(R1b_wout, w_lout, "R1bo")

        def load_mixer1_in():
            load_mat(R1a, w_gin, "R1a")

        def load_mixer1_out():
            load_mat(R1b_wout, w_gout, "R1b")

        class Slot:
            def __init__(self, off, key):
                self.key = key
                self.wg = T(mem, [KC, SL], BF16, off=off)
                self.wu = T(mem, [KC, SL], BF16, off=off + 8192)
                self.wd = T(mem, [4, D], BF16, off=off + 16384)
                self.pgate = T(mem, [KC, D], BF16, off=off)
                self.pproj = T(mem, [2, D], BF16, off=off + 16384)

        slots = [Slot(slot_off[0], "slot0"), Slot(slot_off[1], "slot1")]

        def load_ffn_slice(slot, l, s):
            load_mat(slot.wg, w_up[l, :, s * SL:(s + 1) * SL], slot.key)
            load_mat(slot.wu, w_up[l, :, DFF + s * SL:DFF + (s + 1) * SL], slot.key)
            load_mat(slot.wd, w_dn[l, s * SL:(s + 1) * SL, :], slot.key)

        def load_ple(slot, l):
            load_mat(slot.pgate, w_pg[l], slot.key)
            load_mat(slot.pproj, w_pp[l], slot.key)

        blocks = []
        for gi in range(len(GROUPS)):
            for l in range(2):
                for s in range(NSL):
                    blocks.append(("ffn", l, s))
                blocks.append(("ple", l, None))
        blk_next = [0]

        def issue_next_block():
            j = blk_next[0]
            if j >= len(blocks):
                return
            blk_next[0] += 1
            kind, l, s = blocks[j]
            if kind == "ffn":
                load_ffn_slice(slots[j % 2], l, s)
            else:
                load_ple(slots[j % 2], l)

        def hv(c, c0, n):
            return Hreg.v(c, (c0, c0 + n))

        def hv_tile(c0, n):
            return V(Hreg.ap[:, :, c0:c0 + n], bufs_of([hv(c, c0, n) for c in range(KC)]))

        def hkey(c0):
            k = "H%d" % c0
            if k not in KEY:
                KEY[k] = Buf("key_" + k)
            return KEY[k]

        def rmsnorm(c0, n, gname, outs, rs_pool, final=False):
            ps = psum(n)
            sqs, sd = rs_pool
            for c in range(KC):
                sq = sqs[c % len(sqs)].v((0, n))
                act(sq, hv(c, c0, n), AF.Square)
                mm(ps, onesD.all(), sq, c == 0, c == KC - 1)
            sdv = sd.v((0, n))
            act(sdv, ps, AF.Sqrt, bias=epsT.all())
            emit("dve", lambda e: e.reciprocal(out=sdv.ap, in_=sdv.ap), R=[sdv], W=[sdv])
            for c in range(KC):
                stt(outs[c], hv(c, c0, n), vcol(gname, c), sdv, ALU.mult, ALU.mult)

        first_group = True
        for gi, G in enumerate(GROUPS):
            npz = G["np"]
            ng = npz + (NS if G["sample"] else 0)
            mix_tiles = [(i * 256, 256, "p") for i in range(npz // 256)]
            ffn_tiles = [(i * 512, 512, "p") for i in range(npz // 512)]
            if G["sample"]:
                mix_tiles.append((npz, NS, "s"))
                ffn_tiles.append((npz, NS, "s"))
            last_prompt_group = (G["p0"] + npz == SEQ)

            xv = xT.rearrange("(c p) n -> p c n", p=128)
            for (c0, n, kind) in mix_tiles:
                gcol = G["p0"] + c0 if kind == "p" else SEQ
                hvw = hv_tile(c0, n)
                dma("sp", hvw, xv[:, :, gcol:gcol + n], key=hkey(c0))
            if G["sample"]:
                dma("sp", h0s.all(), st_h0.rearrange("(c p) n -> p c n", p=128), h0s.all())
                dma("sp", lcs.all(), st_lc.rearrange("(c p) (s k) -> p c s k", p=128, k=3), lcs.all())
            if first_group:
                load_mixer0()
            startup_blocks = first_group
            first_group = False

            def load_P(l):
                pv = pT.rearrange("l (c p) n -> l p c n", p=128)
                for k in range(2):
                    dv = Preg.v(k, (0, npz))
                    dma("pool", dv, pv[l, :, k, G["p0"]:G["p0"] + npz], key=KEY["P"])
                    if G["sample"]:
                        dv2 = Preg.v(k, (npz, npz + NS))
                        dma("pool", dv2, pv[l, :, k, SEQ:SEQ + NS], key=KEY["P"])

            for l in range(2):
                arena_reset()
                sqs = [T(mem, [256], BF16) for _ in range(3)]
                sd = T(mem, [256], F32)
                xns = [T(mem, [KC, 256], BF16) for _ in range(2)]
                if l == 0:
                    diagA = T(mem, [4, 4, 128], BF16)
                    diagB = T(mem, [4, 4, 128], BF16, off=Preg.off)

                    def diag_v(j, c):
                        return (diagA if c < 4 else diagB).v(j, c % 4)
                    for j in range(4):
                        for c in range(KC):
                            ts(diag_v(j, c), ident.all(), vcol("lcw", j * 8 + c), ALU.mult, 0.0, ALU.add, eng=POOLENG)
                    gg = [T(mem, [256], BF16) for _ in range(8)]
                    xbuf = [T(mem, [260], BF16, align=64) for _ in range(3)]
                    xbuf_s = [T(mem, [NSEQ, 7], BF16, align=64) for _ in range(2)]
                    xc = [T(mem, [256], BF16) for _ in range(8)]
                    f32s = [T(mem, [256], F32) for _ in range(16)] + [T(mem, [256], F32, off=OFC.off + 1024 * q) for q in range(3)] + [T(mem, [256], F32, off=fcs.off + 1024 * q) for q in range(3)]
                    thr_b = f32s[0:2]
                    av_b = f32s[2:10]
                    a2m_b = f32s[10:14]
                    thi_b = f32s[14:22]
                    t16 = T(mem, [NSEQ], F32, align=64)
                    gated = [T(mem, [KC, 256], BF16) for _ in range(1)]
                    print("LRU arena top", mem.top)
                    xcnt = [0]
                    ntl = len(mix_tiles)

                    def tinfo(ti):
                        c0, n, kind = mix_tiles[ti]
                        smp = kind == "s"
                        last_p = (kind == "p") and last_prompt_group and (c0 + n == npz)
                        return c0, n, smp, last_p

                    def prologue_a(ti):
                        c0, n, smp, last_p = tinfo(ti)
                        ps = psum(n)
                        for c in range(KC):
                            sq = sqs[c % 3].v((0, n))
                            tt(sq, hv(c, c0, n), hv(c, c0, n), ALU.mult, eng=POOLENG)
                            mm(ps, onesD.all(), sq, c == 0, c == KC - 1)
                        return ps

                    def prologue_b(ti, ps):
                        c0, n, smp, last_p = tinfo(ti)
                        xn = xns[ti % 2]
                        sdv = sd.v((0, n))
                        act(sdv, ps, AF.Sqrt, bias=epsT.all())
                        emit("dve", lambda e: e.reciprocal(out=sdv.ap, in_=sdv.ap), R=[sdv], W=[sdv])
                        for c in range(KC):
                            stt(xn.v(c, (0, n)), hv(c, c0, n), vcol("nm0", c), sdv, ALU.mult, ALU.mult)

                    def Xp(ti, half):
                        c0, n, smp, last_p = tinfo(ti)
                        xn = xns[ti % 2]
                        pend = []

                        def xbp(c):
                            i = xcnt[0]
                            xcnt[0] += 1
                            ps = psum(n)
                            for k in range(KC):
                                mm(ps, R1a.v(k, (1024 + c * 128, 1024 + (c + 1) * 128)), xn.v(k, (0, n)), k == 0, k == KC - 1)
                            if not smp:
                                xb = xbuf[i % 3]
                                cp(xb.v((0, 3)), xh_store.v(c, (0, 3)), eng=POOLENG)
                                cp(xb.v((3, 3 + n)), ps, eng="act")
                                cp(xh_store.v(c, (0, 3)), xb.v((n, n + 3)), eng=POOLENG)
                                if last_p:
                                    cp(OLC.v(c, 0), V(ps.ap[:, n - 3:n], ps.bufs), eng="act")
                            else:
                                xb = xbuf_s[i % 2]
                                cp(xb.v(None, (0, 3)), lcs.v(c), eng=POOLENG)
                                ps3 = V(ps.ap.rearrange("p (s t) -> p s t", t=4), ps.bufs)
                                cp(xb.v(None, (3, 7)), ps3, eng="act")
                                cp(OLC.v(c, (1, 17)), V(ps3.ap[:, :, 1:4], ps.bufs), eng="act")
                            return xb

                        def convp(c, xb):
                            ps2 = psum(n)
                            if not smp:
                                for j in range(4):
                                    mm(ps2, diag_v(j, c), xb.v((j, j + n)), j == 0, j == 3)
                            else:
                                ps23 = V(ps2.ap.rearrange("p (s t) -> p s t", t=4), ps2.bufs)
                                for j in range(4):
                                    mm(ps23, diag_v(j, c), xb.v(None, (j, j + 4)), j == 0, j == 3)
                            act(xc[c].v((0, n)), ps2, AF.Identity, bias=vcol("lcb", c))

                        cs = list(range(4 * half, 4 * half + 4))
                        xb0 = xbp(cs[0])
                        xb1 = xbp(cs[1])
                        convp(cs[0], xb0)
                        xb2 = xbp(cs[2])
                        convp(cs[1], xb1)
                        xb3 = xbp(cs[3])
                        convp(cs[2], xb2)
                        convp(cs[3], xb3)

                    def Rp(ti, half):
                        c0, n, smp, last_p = tinfo(ti)
                        for h in (2 * half, 2 * half + 1):
                            for c in (2 * h, 2 * h + 1):
                                cl = c % 2
                                psr = psum(n)
                                for kl in range(2):
                                    mm(psr, R1b_wa.v(2 * h + kl, (cl * 128, (cl + 1) * 128)), xc[2 * h + kl].v((0, n)), kl == 0, kl == 1)
                                psi = psum(n)
                                for kl in range(2):
                                    mm(psi, R1b_wx.v(2 * h + kl, (cl * 128, (cl + 1) * 128)), xc[2 * h + kl].v((0, n)), kl == 0, kl == 1)
                                thr = thr_b[c % 2].v((0, n))
                                act(thr, psr, AF.Tanh, bias=dcol(2, c), scale=0.5)
                                act(thi_b[c].v((0, n)), psi, AF.Tanh, bias=dcol(3, c), scale=0.5)
                                act(av_b[c].v((0, n)), thr, AF.Exp, bias=dcol(0, c), scale=dcol(0, c))

                    def Gp(ti, half):
                        c0, n, smp, last_p = tinfo(ti)
                        xn = xns[ti % 2]
                        for c in range(4 * half, 4 * half + 4):
                            psg = psum(n)
                            for k in range(KC):
                                mm(psg, R1a.v(k, (c * 128, (c + 1) * 128)), xn.v(k, (0, n)), k == 0, k == KC - 1)
                            act(gg[c].v((0, n)), psg, AF.Gelu_apprx_tanh)

                    def ESp(ti, half):
                        c0, n, smp, last_p = tinfo(ti)
                        for c in range(4 * half, 4 * half + 4):
                            av = av_b[c].v((0, n))
                            tt(a2m_b[c % 4].v((0, n)), av, av, ALU.mult, eng=POOLENG)
                            ts(a2m_b[c % 4].v((0, n)), a2m_b[c % 4].v((0, n)), 1.0, ALU.min, 0.0, ALU.max, eng=POOLENG)
                        for c in range(4 * half, 4 * half + 4):
                            a2m = a2m_b[c % 4].v((0, n))
                            act(a2m, a2m, AF.Sqrt, bias=0.25, scale=-0.25)

                    def chain(ti, half):
                        c0, n, smp, last_p = tinfo(ti)
                        gt = gated[0]
                        for c in range(4 * half, 4 * half + 4):
                            hs = a2m_b[c % 4].v((0, n))
                            av = av_b[c].v((0, n))
                            a2m = a2m_b[c % 4].v((0, n))
                            thi = thi_b[c].v((0, n))
                            stt(thi, thi, 1.0, xc[c].v((0, n)), ALU.add, ALU.mult)
                            tt(thi, thi, a2m, ALU.mult)
                            if not smp:
                                emit("dve", lambda e, hs=hs, av=av, thi=thi, c=c: e.tensor_tensor_scan(
                                    out=hs.ap, data0=av.ap, data1=thi.ap, initial=carry.v((c, c + 1)).ap,
                                    op0=ALU.mult, op1=ALU.add), R=[av, thi, carry.v((c, c + 1))], W=[hs])
                                cp(carry.v((c, c + 1)), V(hs.ap[:, n - 1:n], hs.bufs))
                                if last_p:
                                    cp(OH.v(c, (0, 1)), V(hs.ap[:, n - 1:n], hs.bufs))
                            else:
                                a3 = V(av.ap.rearrange("p (s t) -> p s t", t=4)[:, :, 0], av.bufs)
                                b3 = V(thi.ap.rearrange("p (s t) -> p s t", t=4)[:, :, 0], thi.bufs)
                                tt(t16.all(), a3, h0s.v(c), ALU.mult)
                                tt(b3, b3, t16.all(), ALU.add)
                                mset(a3, 0.0)
                                emit("dve", lambda e, hs=hs, av=av, thi=thi: e.tensor_tensor_scan(
                                    out=hs.ap, data0=av.ap, data1=thi.ap, initial=0.0,
                                    op0=ALU.mult, op1=ALU.add), R=[av, thi], W=[hs])
                                cp(OH.v(c, (1, 17)), V(hs.ap.rearrange("p (s t) -> p s t", t=4)[:, :, 3], hs.bufs))
                            tt(gt.v(c, (0, n)), gg[c].v((0, n)), hs, ALU.mult, eng="pool")

                    def Wp(ti):
                        c0, n, smp, last_p = tinfo(ti)
                        gt = gated[0]
                        for m in range(KC):
                            ps = psum(n)
                            for k in range(KC):
                                mm(ps, R1b_wout.v(k, (m * 128, (m + 1) * 128)), gt.v(k, (0, n)), k == 0, k == KC - 1)
                            tt(hv(m, c0, n), hv(m, c0, n), ps, ALU.add)

                    pps = prologue_a(0)
                    if startup_blocks:
                        load_mixer0_rest_a()
                    prologue_b(0, pps)
                    halves = [(ti, hf) for ti in range(ntl) for hf in range(2)]
                    for i, (ti, hf) in enumerate(halves):
                        Xp(ti, hf)
                        if startup_blocks and i == 0:
                            load_mixer0_rest_b()
                        Rp(ti, hf)
                        nxt_ps = None
                        if hf == 0 and ti + 1 < ntl:
                            nxt_ps = prologue_a(ti + 1)
                        if hf == 1 and ti > 0:
                            Wp(ti - 1)
                        if i > 0:
                            ESp(*halves[i - 1])
                            chain(*halves[i - 1])
                        if nxt_ps is not None:
                            prologue_b(ti + 1, nxt_ps)
                        Gp(ti, hf)
                        if startup_blocks and i == 1:
                            issue_next_block()
                            issue_next_block()
                    ESp(*halves[-1])
                    chain(*halves[-1])
                    Wp(ntl - 1)
                    if last_prompt_group:
                        dma("sp", o_h.rearrange("(c p) n -> p c n", p=128), OH.all(), OH.all())
                        dma("sp", o_lc.rearrange("(c p) (s k) -> p c s k", p=128, k=3), OLC.all(), OLC.all())
                        final_bufs.extend(OH.bufs + OLC.bufs)
                    load_mixer1_in()
                else:
                    load_mixer1_out()
                    lng = T(mem, [D], F32)
                    lnb = T(mem, [D], F32)
                    dma("sp", lng.all(), lng_d, lng.all())
                    dma("sp", lnb.all(), lnb_d, lnb.all())
                    wsTm = T(mem, [KC, 128], BF16)
                    bs_hi = T(mem, [1024], BF16, parts=1)
                    bs_lo = T(mem, [1024], BF16, parts=1)
                    bd = T(mem, [KC, 64], BF16, parts=64)
                    tmp_mark = mem.top
                    wsf = T(mem, [KC, 128], F32)
                    dma("sp", wsf.all(), wsT_d.rearrange("p (g t) -> p g t", g=8), wsf.all())
                    for g in range(8):
                        tt(wsTm.v(g), wsf.v(g), cst_f.v((128, 256)), ALU.mult)
                    bsf = T(mem, [1024], F32, parts=1)
                    dma("sp", bsf.all(), bsrow, bsf.all())
                    bs_t = T(mem, [1024], F32, parts=1)
                    cp(bs_hi.all(), bsf.all())
                    cp(bs_t.all(), bs_hi.all())
                    tt(bs_t.all(), bsf.all(), bs_t.all(), ALU.subtract)
                    cp(bs_lo.all(), bs_t.all())
                    if G["sample"]:
                        bdf = T(mem, [KC, 64], F32, parts=64)
                        dma("sp", bdf.all(), bdraw.rearrange("p (g t) -> p g t", g=8), bdf.all())
                        m64t = T(mem, [64], F32, parts=64)
                        dma("sp", m64t.all(), m64, m64t.all())
                        for g in range(8):
                            tt(bd.v(g), bdf.v(g), m64t.all(), ALU.mult)
                    mem.top = tmp_mark
                    dump2("wsTm", V(wsTm.ap.rearrange("p g t -> p (g t)"), wsTm.bufs), [128, 1024], BF16, gi)
                    dump2("bshi", bs_hi.all(), [1, 1024], BF16, gi)
                    dump2("bslo", bs_lo.all(), [1, 1024], BF16, gi)
                    vbuf = [T(mem, [D], F32) for _ in range(2)]
                    vT = [T(mem, [D], BF16) for _ in range(4)]
                    stats = LNS
                    mv = LNM
                    ubuf = [T(mem, [256], F32) for _ in range(2)]
                    gated = [T(mem, [KC, 256], BF16) for _ in range(1)]
                    print("gMLP arena top", mem.top, "stats off", stats.off, "mv off", mv.off, "vbuf", [v.off for v in vbuf], "vT", [v.off for v in vT])
                    qcnt = [0]
                    ntl = len(mix_tiles)
                    vts_of = {}

                    def g_pro(ti):
                        c0, n, kind = mix_tiles[ti]
                        xn = xns[ti % 2]
                        rmsnorm(c0, n, "nm1", [xn.v(c, (0, n)) for c in range(KC)], (sqs, sd))

                    def g_V(ti):
                        c0, n, kind = mix_tiles[ti]
                        smp = kind == "s"
                        xn = xns[ti % 2]
                        nq = 1 if smp else n // 128
                        nt = NS if smp else 128
                        vts = []
                        for q in range(nq):
                            qi = qcnt[0]
                            qcnt[0] += 1
                            pv2 = psum2()
                            for hh in range(2):
                                pvv = V(pv2[hh].ap[0:nt, :], pv2[hh].bufs)
                                for k in range(KC):
                                    mm(pvv, xn.v(k, (q * 128, q * 128 + nt)), R1a.v(k, (1024 + hh * 512, 1024 + (hh + 1) * 512)), k == 0, k == KC - 1)
                            vb = vbuf[qi % 2]
                            for hh in range(2):
                                act(vb.v((hh * 512, (hh + 1) * 512), p=(0, nt)), V(pv2[hh].ap[0:nt, :], pv2[hh].bufs), AF.Gelu_apprx_tanh)
                            for hh in range(2):
                                emit("dve", lambda e, vb=vb, hh=hh, nt=nt: e.bn_stats(out=stats.v((hh * 6, hh * 6 + 6), p=(0, nt)).ap, in_=vb.v((hh * 512, (hh + 1) * 512), p=(0, nt)).ap),
                                     R=[vb.all()], W=[stats.all()])
                            emit("dve", lambda e, nt=nt: e.bn_aggr(out=mv.v((0, 2), p=(0, nt)).ap, in_=stats.v(p=(0, nt)).ap), R=[stats.all()], W=[mv.all()])
                            act(mv.v((2, 3), p=(0, nt)), mv.v((1, 2), p=(0, nt)), AF.Sqrt, bias=epsT.v(p=(0, nt)))
                            emit("dve", lambda e, nt=nt: e.reciprocal(out=mv.v((3, 4), p=(0, nt)).ap, in_=mv.v((2, 3), p=(0, nt)).ap), R=[mv.all()], W=[mv.all()])
                            vbv = vb.v(p=(0, nt))
                            ts(mv.v((2, 3), p=(0, nt)), mv.v((0, 1), p=(0, nt)), mv.v((3, 4), p=(0, nt)), ALU.mult, -1.0, ALU.mult)
                            ts(vbv, vbv, mv.v((3, 4), p=(0, nt)), ALU.mult, mv.v((2, 3), p=(0, nt)), ALU.add, eng="pool")
                            tt(vbv, vbv, lng.v(p=(0, nt)), ALU.mult, eng="pool")
                            vt = vT[qi % 4]
                            if smp:
                                tt(vbv, vbv, lnb.v(p=(0, nt)), ALU.add, eng="pool")
                                cp(vt.v(p=(0, nt)), vbv, eng="act")
                                dma("sp", o_v, vbv, vbv)
                                final_bufs.extend(vb.bufs)
                            else:
                                tt(vt.v(p=(0, nt)), vbv, lnb.v(p=(0, nt)), ALU.add, eng="pool")
                            vts.append(vt)
                        vts_of[ti] = vts

                    def g_S(ti):
                        c0, n, kind = mix_tiles[ti]
                        smp = kind == "s"
                        xn = xns[ti % 2]
                        gt = gated[0]
                        vts = vts_of[ti]
                        nq = 1 if smp else n // 128
                        for g in range(8):
                            psu = psum(n)
                            for k in range(KC):
                                mm(psu, R1a.v(k, (g * 128, (g + 1) * 128)), xn.v(k, (0, n)), k == 0, k == KC - 1)
                            uv = ubuf[g % 2].v((0, n))
                            act(uv, psu, AF.Gelu_apprx_tanh)
                            pss = psum(n)
                            if not smp:
                                for q in range(nq):
                                    sub = V(pss.ap[:, q * 128:(q + 1) * 128], pss.bufs)
                                    mm(sub, ones1.v(p=(0, 1)), bs_hi.v((g * 128, (g + 1) * 128)), q == 0, False)
                                    mm(sub, ones1.v(p=(0, 1)), bs_lo.v((g * 128, (g + 1) * 128)), False, False)
                                    mm(sub, vts[q].v((g * 128, (g + 1) * 128)), wsTm.v(g), False, q == nq - 1)
                            else:
                                hi4 = V(bs_hi.ap[0:1, g * 128:g * 128 + 4].unsqueeze(1).broadcast_to([1, NSEQ, 4]), bs_hi.bufs)
                                lo4 = V(bs_lo.ap[0:1, g * 128:g * 128 + 4].unsqueeze(1).broadcast_to([1, NSEQ, 4]), bs_lo.bufs)
                                ps3 = V(pss.ap.rearrange("p (s t) -> p s t", t=4), pss.bufs)
                                mm(ps3, ones1.v(p=(0, 1)), hi4, True, False)
                                mm(ps3, ones1.v(p=(0, 1)), lo4, False, False)
                                mm(pss, vts[0].v((g * 128, (g + 1) * 128), p=(0, NS)), bd.v(g), False, True)
                            tt(gt.v(g, (0, n)), uv, pss, ALU.mult)

                    def g_W(ti):
                        c0, n, kind = mix_tiles[ti]
                        gt = gated[0]
                        for m in range(KC):
                            ps = psum(n)
                            for k in range(KC):
                                mm(ps, R1b_wout.v(k, (m * 128, (m + 1) * 128)), gt.v(k, (0, n)), k == 0, k == KC - 1)
                            tt(hv(m, c0, n), hv(m, c0, n), ps, ALU.add)

                    g_pro(0)
                    g_V(0)
                    for ti in range(ntl):
                        if ti + 1 < ntl:
                            g_pro(ti + 1)
                            g_V(ti + 1)
                        g_S(ti)
                        g_W(ti)
                    if gi + 1 < len(GROUPS):
                        dma("pool", R1a.v(None, (1024, 2048)), w_lin[:, 1024:2048].rearrange("(k p) n -> p k n", p=128), key=KEY["R1a"])
                        dma("pool", R1a.v(None, (0, 1024)), w_lin[:, 0:1024].rearrange("(k p) n -> p k n", p=128), key=KEY["R1ag"])

                dump(l * 3 + 0, DBG_GROUP, gi, ng)
                if stop_after == ("mix", gi, l):
                    break
                arena_reset()
                sqs = [T(mem, [512], BF16) for _ in range(3)]
                sd = T(mem, [512], F32)
                gbuf = [T(mem, [2 + npz], BF16) for _ in range(4)]
                gbuf_s = [T(mem, [NSEQ, 6], BF16) for _ in range(4)]
                ggb = [T(mem, [512], F32) for _ in range(2)]
                zb = [T(mem, [4, 512], BF16) for _ in range(2)]
                dg = [T(mem, [3, 4, 128], BF16) for _ in range(2)]
                print("FFN arena top", mem.top)
                if G["sample"]:
                    dma("sp", fcs.all(), st_fc[l].rearrange("(c p) (s k) -> p c s k", p=128, k=2), fcs.all())
                for (c0, n, kind) in ffn_tiles:
                    rmsnorm(c0, n, "nf%d" % l, [XNall.v(c, (c0, c0 + n)) for c in range(KC)], (sqs, sd))
                load_P(l)
                def ffn_prep(s):
                    dgs = dg[s % 2]
                    for j in range(3):
                        for cl in range(4):
                            ts(dgs.v(j, cl), ident.all(), vcol("fcw%d" % l, j * 24 + s * 4 + cl), ALU.mult)
                    for cl in range(4):
                        cp(gbuf[cl].v((0, 2)), fh_store.v(l, s * 4 + cl))

                def ffn_U(s, ti, z):
                    bj = (gi * 2 + l) * (NSL + 1) + s
                    slot = slots[bj % 2]
                    dgs = dg[s % 2]
                    c0, n, kind = ffn_tiles[ti]
                    smp = kind == "s"
                    last_p = (kind == "p") and last_prompt_group and (c0 + n == npz)
                    psus = {}

                    def Gp(cl):
                        cg = s * 4 + cl
                        psg = psum(n)
                        for k in range(KC):
                            mm(psg, slot.wg.v(k, (cl * 128, (cl + 1) * 128)), XNall.v(k, (c0, c0 + n)), k == 0, k == KC - 1)
                        if not smp:
                            cp(gbuf[cl].v((2 + c0, 2 + c0 + n)), psg, eng="act")
                            if last_p:
                                cp(OFC.v(cg, 0), V(psg.ap[:, n - 2:n], psg.bufs), eng="act")
                            if c0 + n == npz:
                                cp(fh_store.v(l, cg), gbuf[cl].v((npz, npz + 2)))
                        else:
                            gs = gbuf_s[cl]
                            cp(gs.v(None, (0, 2)), fcs.v(cg))
                            psg3 = V(psg.ap.rearrange("p (s t) -> p s t", t=4), psg.bufs)
                            cp(gs.v(None, (2, 6)), psg3, eng="act")
                            cp(OFC.v(cg, (1, 17)), V(psg3.ap[:, :, 2:4], psg.bufs), eng="act")

                    def Up(cl):
                        psu = psum(n)
                        for k in range(KC):
                            mm(psu, slot.wu.v(k, (cl * 128, (cl + 1) * 128)), XNall.v(k, (c0, c0 + n)), k == 0, k == KC - 1)
                        psus[cl] = psu

                    def Cp(cl):
                        cg = s * 4 + cl
                        psc = psum(n)
                        if not smp:
                            for j in range(3):
                                mm(psc, dgs.v(j, cl), gbuf[cl].v((c0 + j, c0 + j + n)), j == 0, j == 2)
                        else:
                            gs = gbuf_s[cl]
                            psc3 = V(psc.ap.rearrange("p (s t) -> p s t", t=4), psc.bufs)
                            for j in range(3):
                                mm(psc3, dgs.v(j, cl), gs.v(None, (j, j + 4)), j == 0, j == 2)
                        ggv = ggb[cl % 2].v((0, n))
                        act(ggv, psc, AF.Gelu_apprx_tanh, bias=vcol("fcb%d" % l, cg))
                        tt(z.v(cl, (0, n)), ggv, psus[cl], ALU.mult)

                    Gp(0); Up(0); Gp(1); Cp(0); Up(1); Gp(2); Cp(1); Up(2); Gp(3); Cp(2); Up(3); Cp(3)

                def ffn_D(s, ti, z):
                    bj = (gi * 2 + l) * (NSL + 1) + s
                    slot = slots[bj % 2]
                    c0, n, kind = ffn_tiles[ti]
                    for m in range(KC):
                        ps = psum(n)
                        for k in range(4):
                            mm(ps, slot.wd.v(k, (m * 128, (m + 1) * 128)), z.v(k, (0, n)), k == 0, k == 3)
                        tt(hv(m, c0, n), hv(m, c0, n), ps, ALU.add)

                items = [(s, ti) for s in range(NSL) for ti in range(len(ffn_tiles))]
                prev = None
                for idx, (s, ti) in enumerate(items):
                    if ti == 0:
                        ffn_prep(s)
                    z = zb[idx % 2]
                    ffn_U(s, ti, z)
                    if prev is not None:
                        ffn_D(*prev)
                        if prev[1] == len(ffn_tiles) - 1:
                            issue_next_block()
                    prev = (s, ti, z)
                ffn_D(*prev)
                issue_next_block()
                if last_prompt_group:
                    dma("sp", o_fc[l].rearrange("(c p) (s k) -> p c s k", p=128, k=2), OFC.all(), OFC.all())
                    final_bufs.extend(OFC.bufs)
                if l == 1 and gi + 1 < len(GROUPS):
                    load_mat(R1b_wa, w_la, "R1b")
                    load_mat(R1b_wx, w_lx, "R1b")
                    load_mat(R1b_wout, w_lout, "R1bo")

                dump(l * 3 + 1, DBG_GROUP, gi, ng)
                if stop_after == ("ffn", gi, l):
                    break
                arena_reset()
                sqs = [T(mem, [512], BF16) for _ in range(3)]
                sd = T(mem, [512], F32)
                sqs2 = [T(mem, [512], BF16) for _ in range(2)]
                sd2 = T(mem, [512], F32)
                xns = [T(mem, [KC, 512], BF16) for _ in range(2)]
                thb = [T(mem, [512], F32) for _ in range(2)]
                tb = [T(mem, [512], F32) for _ in range(2)]
                bj = (gi * 2 + l) * (NSL + 1) + NSL
                slot = slots[bj % 2]
                ptiles = ffn_tiles
                yv = yT.rearrange("(c p) n -> p c n", p=128)

                def p_pro(ti):
                    c0, n, kind = ptiles[ti]
                    xn = xns[ti % 2]
                    rmsnorm(c0, n, "np%d" % l, [xn.v(c, (0, n)) for c in range(KC)], (sqs, sd))

                def p_body(ti, ms):
                    c0, n, kind = ptiles[ti]
                    xn = xns[ti % 2]
                    for m in ms:
                        psg = psum(n)
                        for k in range(KC):
                            mm(psg, slot.pgate.v(k, (m * 128, (m + 1) * 128)), xn.v(k, (0, n)), k == 0, k == KC - 1)
                        th = thb[m % 2].v((0, n))
                        act(th, psg, AF.Sigmoid)
                        psp = psum(n)
                        for k in range(2):
                            mm(psp, slot.pproj.v(k, (m * 128, (m + 1) * 128)), Preg.v(k, (c0, c0 + n)), k == 0, k == 1)
                        tv = tb[m % 2].v((0, n))
                        tt(tv, th, psp, ALU.mult)
                        tt(hv(m, c0, n), hv(m, c0, n), tv, ALU.add, eng="pool")

                def p_final(ti):
                    c0, n, kind = ptiles[ti]
                    rmsnorm(c0, n, "nfin", [hv(c, c0, n) for c in range(KC)], (sqs2, sd2))
                    gcol = G["p0"] + c0 if kind == "p" else SEQ
                    hvr = hv_tile(c0, n)
                    dma("sp", yv[:, :, gcol:gcol + n], hvr, key=hkey(c0))
                    final_bufs.append(hkey(c0))

                p_pro(0)
                for ti in range(len(ptiles)):
                    p_body(ti, range(0, 2))
                    if ti + 1 < len(ptiles):
                        p_pro(ti + 1)
                    p_body(ti, range(2, KC))
                    if l == 1 and ti > 0:
                        p_final(ti - 1)
                if l == 1:
                    p_final(len(ptiles) - 1)
                issue_next_block()
                dump(l * 3 + 2, DBG_GROUP, gi, ng)
            else:
                continue
            break
        P.finalize(final_wait_bufs=final_bufs)
        nops = {e: len(P.ops[e]) for e in ENGS}
        print("ops per engine", nops)
    return nc


_NC_CACHE = {}
_DBG = {}


def _chunks(v, n):
    return np.ascontiguousarray(np.asarray(v, np.float32).reshape(n, 128).T)


def kernel(x_prompt, x_sample, p_prompt, p_sample, state_lru_h, state_lru_conv, state_ffn_conv,
           norm_mix, norm_ffn, norm_ple, norm_final,
           lru_w_in, lru_conv_w, lru_conv_b, lru_w_a, lru_b_a, lru_w_x, lru_b_x, lru_lambda, lru_w_out,
           gm_w_in, gm_ln_g, gm_ln_b, gm_w_s, gm_b_s, gm_w_out,
           ffn_w_up, ffn_conv_w, ffn_conv_b, ffn_w_down,
           ple_w_gate, ple_w_proj, _stop_after=None, _debug=False):
    f = lambda a: np.asarray(a, np.float32)
    x_prompt, x_sample, p_prompt, p_sample = f(x_prompt), f(x_sample), f(p_prompt), f(p_sample)
    state_lru_h, state_lru_conv, state_ffn_conv = f(state_lru_h), f(state_lru_conv), f(state_ffn_conv)
    n_cores = 8
    cols = [_chunks(f(norm_mix)[0], 8), _chunks(f(norm_mix)[1], 8), _chunks(f(norm_ffn)[0], 8), _chunks(f(norm_ffn)[1], 8),
            _chunks(f(norm_ple)[0], 8), _chunks(f(norm_ple)[1], 8), _chunks(f(norm_final), 8)]
    cols += [_chunks(f(lru_conv_w)[0, j], 8) for j in range(4)]
    cols += [_chunks(f(lru_conv_b)[0], 8), _chunks(f(lru_b_a)[0], 8), _chunks(f(lru_b_x)[0], 8), _chunks(f(lru_lambda)[0], 8)]
    for l in range(2):
        cols += [_chunks(f(ffn_conv_w)[l, j], 24) for j in range(3)]
    cols += [_chunks(f(ffn_conv_b)[0], 24), _chunks(f(ffn_conv_b)[1], 24)]
    vecs = np.ascontiguousarray(np.concatenate(cols, axis=1))
    assert vecs.shape == (128, NV), vecs.shape
    ident = np.eye(128, dtype=np.float32)
    mask128 = np.triu(np.ones((128, 128), np.float32))
    cst = np.ascontiguousarray(np.concatenate([ident, mask128], axis=1))
    m64 = np.zeros((64, 64), np.float32)
    for q in range(16):
        m64[q * 4:(q + 1) * 4, q * 4:(q + 1) * 4] = np.triu(np.ones((4, 4), np.float32))
    ws = f(gm_w_s)[0]
    wsT = np.ascontiguousarray(ws.transpose(2, 0, 1).reshape(128, 1024))
    blk = ws[:, 0:4, 0:4].transpose(2, 0, 1)
    bdraw = np.ascontiguousarray(np.tile(blk[None, :, :, None, :], (16, 1, 1, 16, 1)).reshape(64, 8 * 64))
    bsrow = np.ascontiguousarray(f(gm_b_s)[0].reshape(1, 1024))
    lng_bc = np.ascontiguousarray(np.tile(f(gm_ln_g)[0][None, :], (128, 1)))
    lnb_bc = np.ascontiguousarray(np.tile(f(gm_ln_b)[0][None, :], (128, 1)))
    shared = {
        "vecs": vecs, "cst": cst, "m64": m64, "bdraw": bdraw, "bsrow": bsrow,
        "lng_bc": lng_bc, "lnb_bc": lnb_bc, "wsT": wsT,
        "lru_w_in": np.ascontiguousarray(f(lru_w_in)[0]),
        "lru_w_a": np.ascontiguousarray(f(lru_w_a)[0].reshape(1024, 256)),
        "lru_w_x": np.ascontiguousarray(f(lru_w_x)[0].reshape(1024, 256)),
        "lru_w_out": np.ascontiguousarray(f(lru_w_out)[0]),
        "gm_w_in": np.ascontiguousarray(f(gm_w_in)[0]),
        "gm_w_out": np.ascontiguousarray(f(gm_w_out)[0]),
        "ffn_w_up": np.ascontiguousarray(f(ffn_w_up)),
        "ffn_w_down": np.ascontiguousarray(f(ffn_w_down)),
        "ple_w_gate": np.ascontiguousarray(f(ple_w_gate)),
        "ple_w_proj": np.ascontiguousarray(f(ple_w_proj)),
    }
    in_maps = []
    for i in range(n_cores):
        sq = slice(16 * i, 16 * i + 16)
        xT = np.concatenate([x_prompt[i].T, x_sample[sq].reshape(64, 1024).T], axis=1)
        pT = np.stack([np.concatenate([p_prompt[l, i].T, p_sample[l, sq].reshape(64, 256).T], axis=1) for l in range(2)])
        m = dict(shared)
        m["xT"] = np.ascontiguousarray(xT)
        m["pT"] = np.ascontiguousarray(pT)
        m["st_h0"] = np.ascontiguousarray(state_lru_h[0, sq].T)
        m["st_lc"] = np.ascontiguousarray(state_lru_conv[0, sq].transpose(2, 0, 1).reshape(1024, 48))
        m["st_fc"] = np.ascontiguousarray(state_ffn_conv[:, sq].transpose(0, 3, 1, 2).reshape(2, 3072, 32))
        in_maps.append(m)
    key = repr((_stop_after, _debug))
    if key not in _NC_CACHE:
        _NC_CACHE[key] = build_program(_stop_after, _debug)
    nc = _NC_CACHE[key]
    res = run_bass_kernel_spmd(nc, in_maps, core_ids=list(range(n_cores)))
    R = res.results
    if _debug:
        _DBG.clear()
        _DBG["dbg"] = np.stack([R[i]["dbg"] for i in range(n_cores)])
        _DBG.update({k: np.asarray(R[0][k]).astype(np.float32) for k in R[0] if k.startswith("dbg_")})
    y_prompt = np.stack([R[i]["yT"][:, :SEQ].T for i in range(n_cores)])
    y_sample = np.concatenate([R[i]["yT"][:, SEQ:].T.reshape(16, 4, 1024) for i in range(n_cores)])
    oh = [R[i]["o_h"] for i in range(n_cores)]
    olc = [R[i]["o_lc"].reshape(1024, 17, 3) for i in range(n_cores)]
    ofc = [R[i]["o_fc"].reshape(2, 3072, 17, 2) for i in range(n_cores)]
    new_lru_h_prompt = np.stack([oh[i][:, 0] for i in range(n_cores)])[None]
    new_lru_h_sample = np.concatenate([oh[i][:, 1:].T for i in range(n_cores)])[None]
    new_lru_conv_prompt = np.stack([olc[i][:, 0, :].T for i in range(n_cores)])[None]
    new_lru_conv_sample = np.concatenate([olc[i][:, 1:, :].transpose(1, 2, 0) for i in range(n_cores)])[None]
    new_ffn_conv_prompt = np.stack([ofc[i][:, :, 0, :].transpose(0, 2, 1) for i in range(n_cores)], axis=1)
    new_ffn_conv_sample = np.concatenate([ofc[i][:, :, 1:, :].transpose(0, 2, 3, 1) for i in range(n_cores)], axis=1)
    new_gm_v_sample = np.concatenate([R[i]["o_v"].reshape(16, 4, 1024) for i in range(n_cores)])[None]
    outs = (y_prompt, y_sample, new_lru_h_prompt, new_lru_conv_prompt, new_ffn_conv_prompt,
            new_lru_h_sample, new_lru_conv_sample, new_ffn_conv_sample, new_gm_v_sample)
    return tuple(np.ascontiguousarray(o, dtype=np.float32) for o in outs)
```

```python
import numpy as np
from contextlib import ExitStack
import concourse.bass as bass
import concourse.mybir as mybir
from concourse.bass_utils import run_bass_kernel_spmd

F32 = mybir.dt.float32
BF16 = mybir.dt.bfloat16
AF = mybir.ActivationFunctionType
ALU = mybir.AluOpType

ENGS = ("pe", "act", "dve", "pool", "sp")
PAGE = 512
SB_BYTES = 212480
EPS = 1e-6


class Buf:
    __slots__ = ("name", "w", "rs", "dsem", "dcnt")

    def __init__(self, name):
        self.name = name
        self.w = None
        self.rs = {}
        self.dsem = None
        self.dcnt = 0


class Op:
    __slots__ = ("eng", "idx", "fn", "deps", "is_dma", "signal", "ticket", "dbuf", "dval")

    def __init__(self, eng, idx, fn, is_dma):
        self.eng = eng
        self.idx = idx
        self.fn = fn
        self.deps = None
        self.is_dma = is_dma
        self.signal = False
        self.ticket = None
        self.dbuf = None
        self.dval = None


class Prog:
    def __init__(self, nc, same_eng_dist=3):
        self.nc = nc
        self.ops = {e: [] for e in ENGS}
        self.same_eng_dist = same_eng_dist
        self.dma_bufs = []

    def add(self, eng, fn, reads=(), writes=(), dma=False, key=None):
        lst = self.ops[eng]
        op = Op(eng, len(lst), fn, dma)
        best = {}
        ddeps = {}

        def consider(d):
            if d is None or d is op:
                return
            if d.is_dma:
                ddeps[id(d.dbuf)] = (d.dbuf, 16 * d.dbuf.dcnt)
                return
            if d.eng == eng and not dma:
                if eng == "pe":
                    return
                if op.idx - d.idx > self.same_eng_dist:
                    return
            cur = best.get(d.eng)
            if cur is None or cur.idx < d.idx:
                best[d.eng] = d

        for b in reads:
            consider(b.w)
        for b in writes:
            consider(b.w)
            for r in b.rs.values():
                consider(r)
        op.deps = list(best.values()) + list(ddeps.values())
        if dma:
            if key is None:
                key = (list(writes) + list(reads))[0]
            if key.dsem is None:
                key.dsem = -1
                self.dma_bufs.append(key)
            key.dcnt += 1
            op.dbuf = key
            op.dval = 16 * key.dcnt
        for b in reads:
            if dma:
                b.rs[("dma", id(op))] = op
            else:
                b.rs[eng] = op
        for b in writes:
            b.w = op
            b.rs = {}
        lst.append(op)
        return op

    def finalize(self, final_wait_bufs=()):
        nc = self.nc
        for e in ENGS:
            for op in self.ops[e]:
                for d in op.deps:
                    if not isinstance(d, tuple):
                        d.signal = True
        for e in ENGS:
            t = 0
            for op in self.ops[e]:
                if op.signal and not op.is_dma:
                    t += 1
                    op.ticket = t
        with ExitStack() as st:
            esem = {e: st.enter_context(nc.semaphore("s_" + e)) for e in ENGS}
            for i, b in enumerate(self.dma_bufs):
                b.dsem = st.enter_context(nc.semaphore("d%d" % i))
            block = st.enter_context(nc.Block())
            engobj = {"pe": block.tensor, "act": block.scalar, "dve": block.vector,
                      "pool": block.gpsimd, "sp": block.sync}
            prog = self

            def make(e):
                def body(eng):
                    seen = {}
                    for op in prog.ops[e]:
                        need = {}
                        for d in op.deps:
                            if isinstance(d, tuple):
                                k, v = d[0].dsem, d[1]
                            else:
                                k, v = esem[d.eng], d.ticket
                            kk = id(k)
                            if kk not in need or need[kk][1] < v:
                                need[kk] = (k, v)
                        for kk, (k, v) in need.items():
                            if seen.get(kk, 0) >= v:
                                continue
                            seen[kk] = v
                            eng.wait_ge(k, v)
                        ins = op.fn(eng)
                        if op.is_dma:
                            ins.then_inc(op.dbuf.dsem, 16)
                        elif op.signal:
                            ins.then_inc(esem[e], 1)
                    if e == "sp":
                        done = set()
                        for b in final_wait_bufs:
                            if b.dsem is not None and b.dcnt > 0 and id(b) not in done:
                                done.add(id(b))
                                eng.wait_ge(b.dsem, 16 * b.dcnt)
                return body

            for e in ENGS:
                engobj[e](make(e))


class V:
    __slots__ = ("ap", "bufs")

    def __init__(self, ap, bufs):
        self.ap = ap
        self.bufs = bufs


class Mem:
    def __init__(self, SB):
        self.SB = SB
        self.pages = [Buf("pg%d" % i) for i in range(SB_BYTES // PAGE)]
        self.top = 0

    def alloc(self, nbytes, align=PAGE):
        off = (self.top + align - 1) // align * align
        self.top = off + nbytes
        assert self.top <= SB_BYTES, ("SBUF overflow", self.top)
        return off

    def pg(self, off, nbytes):
        return self.pages[off // PAGE:(off + nbytes - 1) // PAGE + 1]


class T:
    def __init__(self, mem, shape, dtype, off=None, align=PAGE, parts=128):
        self.mem = mem
        self.dtype = dtype
        self.shape = tuple(shape)
        self.esz = 4 if dtype == F32 else 2
        n = 1
        for s in self.shape:
            n *= s
        nb = (n * self.esz + 3) // 4 * 4
        if off is None:
            off = mem.alloc(nb, align)
        assert off % 4 == 0
        self.off = off
        self.nbytes = nb
        base = mem.SB[0:parts, off // 4:off // 4 + nb // 4]
        if dtype != F32:
            base = base.bitcast(dtype)
            if base.shape[1] != n:
                base = base[:, 0:n]
        if len(self.shape) == 2:
            base = base.rearrange("p (a b) -> p a b", a=self.shape[0])
        elif len(self.shape) == 3:
            base = base.rearrange("p (a b c) -> p a b c", a=self.shape[0], b=self.shape[1])
        elif len(self.shape) == 4:
            base = base.rearrange("p (a b c d) -> p a b c d", a=self.shape[0], b=self.shape[1], c=self.shape[2])
        self.ap = base
        self.bufs = mem.pg(off, nb)
        st = []
        acc = 1
        for s in reversed(self.shape):
            st.append(acc)
            acc *= s
        self.strides = tuple(reversed(st))

    def v(self, *idx, p=None):
        key = [slice(None) if p is None else slice(p[0], p[1])]
        lo = 0
        hi = 0
        for d, s in enumerate(self.shape):
            i = idx[d] if d < len(idx) else None
            if i is None:
                key.append(slice(None))
                hi += (s - 1) * self.strides[d]
            elif isinstance(i, tuple):
                key.append(slice(i[0], i[1]))
                lo += i[0] * self.strides[d]
                hi += (i[1] - 1) * self.strides[d]
            else:
                key.append(i)
                lo += i * self.strides[d]
                hi += i * self.strides[d]
        ap = self.ap[tuple(key)]
        return V(ap, self.mem.pg(self.off + lo * self.esz, (hi - lo + 1) * self.esz))

    def all(self):
        return V(self.ap, self.bufs)


def bufs_of(vs):
    out = []
    seen = set()
    for v in vs:
        for b in v.bufs:
            if id(b) not in seen:
                seen.add(id(b))
                out.append(b)
    return out


D = 1024
KC = 8
SEQ = 2048
NS = 64
NSEQ = 16
NTOK = SEQ + NS
DFF = 3072
SL = 512
NSL = DFF // SL
HROW = 1152

VC = {}
_o = 0
for _n, _w in [("nm0", 8), ("nm1", 8), ("nf0", 8), ("nf1", 8), ("np0", 8), ("np1", 8), ("nfin", 8),
               ("lcw", 32), ("lcb", 8), ("ba", 8), ("bx", 8), ("lam", 8),
               ("fcw0", 72), ("fcw1", 72), ("fcb0", 24), ("fcb1", 24)]:
    VC[_n] = _o
    _o += _w
NV = _o

DBG_GROUP = 0
POOLENG = "pool"
GROUPS = [
    dict(p0=0, np=1024, sample=False),
    dict(p0=1024, np=1024, sample=True),
]


def build_program(stop_after=None, debug=False):
    nc = bass.Bass("TRN2", target_bir_lowering=False)

    def din(name, shape):
        return nc.dram_tensor(name, list(shape), F32, kind="ExternalInput").ap()

    def dout(name, shape):
        return nc.dram_tensor(name, list(shape), F32, kind="ExternalOutput").ap()

    xT = din("xT", [D, NTOK])
    pT = din("pT", [2, 256, NTOK])
    st_h0 = din("st_h0", [D, NSEQ])
    st_lc = din("st_lc", [D, NSEQ * 3])
    st_fc = din("st_fc", [2, DFF, NSEQ * 2])
    vecs = din("vecs", [128, NV])
    cst = din("cst", [128, 256])
    m64 = din("m64", [64, 64])
    bdraw = din("bdraw", [64, 512])
    bsrow = din("bsrow", [1, 1024])
    lng_d = din("lng_bc", [128, D])
    lnb_d = din("lnb_bc", [128, D])
    wsT_d = din("wsT", [128, 1024])
    w_lin = din("lru_w_in", [D, 2048])
    w_la = din("lru_w_a", [D, 256])
    w_lx = din("lru_w_x", [D, 256])
    w_lout = din("lru_w_out", [D, D])
    w_gin = din("gm_w_in", [D, 2048])
    w_gout = din("gm_w_out", [D, D])
    w_up = din("ffn_w_up", [2, D, 2 * DFF])
    w_dn = din("ffn_w_down", [2, DFF, D])
    w_pg = din("ple_w_gate", [2, D, D])
    w_pp = din("ple_w_proj", [2, 256, D])

    yT = dout("yT", [D, NTOK])
    o_h = dout("o_h", [D, 17])
    o_lc = dout("o_lc", [D, 51])
    o_fc = dout("o_fc", [2, DFF, 34])
    o_v = dout("o_v", [NS, D])
    dbg = dout("dbg", [6, D, 1088]) if debug else None

    with ExitStack() as st:
        SB = st.enter_context(nc.sbuf_tensor("SB", [128, SB_BYTES // 4], F32))
        PS = st.enter_context(nc.psum_tensor("PS", [128, 8, 512], F32))
        P = Prog(nc)
        mem = Mem(SB)
        psb = [Buf("ps%d" % i) for i in range(8)]
        ps_rr = [0]

        def psum(n=512):
            b = ps_rr[0]
            ps_rr[0] = (b + 1) % 8
            return V(PS[:, b, 0:n], [psb[b]])

        def psum2():
            b = ps_rr[0]
            if b % 2:
                b = (b + 1) % 8
            ps_rr[0] = (b + 2) % 8
            return [V(PS[:, b, :], [psb[b]]), V(PS[:, b + 1, :], [psb[b + 1]])]

        def emit(eng, fn, R=(), W=(), dma=False, key=None):
            return P.add(eng, fn, reads=bufs_of(R), writes=bufs_of(W), dma=dma, key=key)

        def act(out, in_, func, bias=None, scale=None, R=(), eng="act"):
            kw = {}
            rr = [in_] + list(R)
            if bias is not None:
                kw["bias"] = bias.ap if isinstance(bias, V) else bias
                if isinstance(bias, V):
                    rr.append(bias)
            if scale is not None:
                kw["scale"] = scale.ap if isinstance(scale, V) else scale
                if isinstance(scale, V):
                    rr.append(scale)
            emit("act", lambda e: e.activation(out=out.ap, in_=in_.ap, func=func, **kw), R=rr, W=[out])

        def tt(out, a, b, op, eng="dve"):
            emit(eng, lambda e: e.tensor_tensor(out=out.ap, in0=a.ap, in1=b.ap, op=op), R=[a, b], W=[out])

        def ts(out, a, s1, op0, s2=None, op1=None, eng="dve"):
            rr = [a]
            s1a = s1
            s2a = s2
            if isinstance(s1, V):
                rr.append(s1)
                s1a = s1.ap
            if isinstance(s2, V):
                rr.append(s2)
                s2a = s2.ap
            if op1 is None:
                emit(eng, lambda e: e.tensor_scalar(out=out.ap, in0=a.ap, scalar1=s1a, scalar2=None, op0=op0), R=rr, W=[out])
            else:
                emit(eng, lambda e: e.tensor_scalar(out=out.ap, in0=a.ap, scalar1=s1a, scalar2=s2a, op0=op0, op1=op1), R=rr, W=[out])

        def stt(out, a, s, b, op0, op1):
            rr = [a, b]
            sa = s
            if isinstance(s, V):
                rr.append(s)
                sa = s.ap
            emit("dve", lambda e: e.scalar_tensor_tensor(out=out.ap, in0=a.ap, scalar=sa, in1=b.ap, op0=op0, op1=op1), R=rr, W=[out])

        def cp(out, in_, eng="dve"):
            if eng == "act":
                emit("act", lambda e: e.activation(out=out.ap, in_=in_.ap, func=AF.Copy), R=[in_], W=[out])
            else:
                emit(eng, lambda e: e.tensor_copy(out=out.ap, in_=in_.ap), R=[in_], W=[out])

        def mset(out, val, eng="dve"):
            emit(eng, lambda e: e.memset(out.ap, val), W=[out])

        def mm(ps, lhsT, rhs, start, stop, R=()):
            emit("pe", lambda e: e.matmul(ps.ap, lhsT=lhsT.ap, rhs=rhs.ap, start=start, stop=stop),
                 R=[lhsT, rhs] + list(R), W=[ps])

        def dma(eng, out, in_, sb_side=None, key=None):
            if isinstance(out, V):
                emit(eng, lambda e: e.dma_start(out=out.ap, in_=in_), W=[out], dma=True, key=key)
            else:
                emit(eng, lambda e: e.dma_start(out=out, in_=in_.ap), R=[in_], dma=True, key=key)

        Hreg = T(mem, [KC, HROW], F32)
        Preg = T(mem, [2, HROW], BF16)
        R1a = T(mem, [KC, 2048], BF16)
        R1b_off = mem.alloc(24576)
        slot_off = [mem.alloc(24576), mem.alloc(24576)]
        VEC = T(mem, [NV], F32)
        DER = T(mem, [6, 8], F32, align=64)
        onesD = T(mem, [128], BF16, align=64)
        ones1 = T(mem, [128], BF16, align=64)
        ident = T(mem, [128], BF16, align=64)
        epsT = T(mem, [1], F32, align=64)
        cst_f = T(mem, [256], F32, align=64)
        carry = T(mem, [8], F32)
        LNS = T(mem, [12], F32, align=64)
        LNM = T(mem, [4], F32, align=64)
        xh_store = T(mem, [8, 4], BF16)
        fh_store = T(mem, [2, 24, 2], BF16, align=64)
        h0s = T(mem, [8, NSEQ], F32)
        lcs = T(mem, [8, NSEQ, 3], F32, align=64)
        fcs = T(mem, [24, NSEQ, 2], F32)
        OH = T(mem, [8, 17], F32)
        OLC = T(mem, [8, 17, 3], F32, align=64)
        OFC = T(mem, [24, 17, 2], F32)
        arena0 = (mem.top + PAGE - 1) // PAGE * PAGE
        print("fixed bytes", arena0, "arena", SB_BYTES - arena0)

        def arena_reset():
            mem.top = arena0

        dbg2 = {}

        def dump2(name, v, shape, dtype, gi):
            if debug and gi == DBG_GROUP and name not in dbg2:
                t = nc.dram_tensor("dbg_" + name, list(shape), F32, kind="ExternalOutput").ap()
                dbg2[name] = t
                dma("pool", t, v)
                final_bufs.extend(v.bufs)

        def dump(idx, gi_want, gi, ng):
            if debug and gi == gi_want:
                hvr = Hreg.v(None, (0, ng))
                dma("sp", dbg[idx].rearrange("(c p) n -> p c n", p=128)[:, :, 0:ng], hvr, key=KEY["H"])
                final_bufs.append(KEY["H"])

        final_bufs = []
        dma("sp", VEC.all(), vecs, VEC.all())
        dma("sp", cst_f.all(), cst, cst_f.all())
        cp(ident.all(), cst_f.v((0, 128)))
        mset(onesD.all(), 1.0 / 1024.0)
        mset(ones1.all(), 1.0)
        mset(epsT.all(), EPS)
        mset(carry.all(), 0.0)
        mset(xh_store.all(), 0.0)
        mset(fh_store.all(), 0.0)
        lam = VEC.v((VC["lam"], VC["lam"] + 8))
        act(DER.v(4), lam, AF.Exp, scale=-1.0)
        act(DER.v(5), DER.v(4), AF.Ln, bias=1.0)
        ts(DER.v(0), DER.v(5), -4.0, ALU.mult)
        ts(DER.v(1), DER.v(5), -8.0, ALU.mult)
        ts(DER.v(2), VEC.v((VC["ba"], VC["ba"] + 8)), 0.5, ALU.mult)
        ts(DER.v(3), VEC.v((VC["bx"], VC["bx"] + 8)), 0.5, ALU.mult)

        def vcol(name, c):
            return VEC.v((VC[name] + c, VC[name] + c + 1))

        def dcol(r, c):
            return DER.v(r, (c, c + 1))

        KEY = {n: Buf("key_" + n) for n in ("R1a", "R1ag", "R1b", "R1bo", "slot0", "slot1", "P", "H")}

        def load_w(dst_T, kidx, src_rows_ap, key):
            dv = dst_T.v(kidx)
            dma("pool", dv, src_rows_ap, key=KEY[key])

        def rows(w2d, k):
            return w2d[k * 128:(k + 1) * 128, :]

        def load_mat(dst_T, src2d, key):
            dma("pool", dst_T.all(), src2d.rearrange("(k p) n -> p k n", p=128), key=KEY[key])

        R1b_wa = T(mem, [KC, 256], BF16, off=R1b_off)
        R1b_wx = T(mem, [KC, 256], BF16, off=R1b_off + 4096)
        R1b_wout = T(mem, [KC, D], BF16, off=R1b_off + 8192)
        XNall = T(mem, [KC, 1088], BF16, off=R1b_off)

        def load_lru_weights():
            dma("pool", R1a.v(None, (1024, 2048)), w_lin[:, 1024:2048].rearrange("(k p) n -> p k n", p=128), key=KEY["R1a"])
            load_mat(R1b_wa, w_la, "R1b")
            load_mat(R1b_wx, w_lx, "R1b")
            dma("pool", R1a.v(None, (0, 1024)), w_lin[:, 0:1024].rearrange("(k p) n -> p k n", p=128), key=KEY["R1ag"])
            load_mat(R1b_wout, w_lout, "R1bo")

        def load_mixer0():
            dma("pool", R1a.v(None, (1024, 2048)), w_lin[:, 1024:2048].rearrange("(k p) n -> p k n", p=128), key=KEY["R1a"])

        def load_mixer0_rest_a():
            load_mat(R1b_wa, w_la, "R1b")
            load_mat(R1b_wx, w_lx, "R1b")

        def load_mixer0_rest_b():
            dma("pool", R1a.v(None, (0, 1024)), w_lin[:, 0:1024].rearrange("(k p) n -> p k n", p=128), key=KEY["R1ag"])
            load_mat(R1b_wout, w_lout, "R1bo")

        def load_mixer1_in():
            load_mat(R1a, w_gin, "R1a")

        def load_mixer1_out():
            load_mat(R1b_wout, w_gout, "R1b")

        class Slot:
            def __init__(self, off, key):
                self.key = key
                self.wg = T(mem, [KC, SL], BF16, off=off)
                self.wu = T(mem, [KC, SL], BF16, off=off + 8192)
                self.wd = T(mem, [4, D], BF16, off=off + 16384)
                self.pgate = T(mem, [KC, D], BF16, off=off)
                self.pproj = T(mem, [2, D], BF16, off=off + 16384)

        slots = [Slot(slot_off[0], "slot0"), Slot(slot_off[1], "slot1")]

        def load_ffn_slice(slot, l, s):
            load_mat(slot.wg, w_up[l, :, s * SL:(s + 1) * SL], slot.key)
            load_mat(slot.wu, w_up[l, :, DFF + s * SL:DFF + (s + 1) * SL], slot.key)
            load_mat(slot.wd, w_dn[l, s * SL:(s + 1) * SL, :], slot.key)

        def load_ple(slot, l):
            load_mat(slot.pgate, w_pg[l], slot.key)
            load_mat(slot.pproj, w_pp[l], slot.key)

        blocks = []
        for gi in range(len(GROUPS)):
            for l in range(2):
                for s in range(NSL):
                    blocks.append(("ffn", l, s))
                blocks.append(("ple", l, None))
        blk_next = [0]

        def issue_next_block():
            j = blk_next[0]
            if j >= len(blocks):
                return
            blk_next[0] += 1
            kind, l, s = blocks[j]
            if kind == "ffn":
                load_ffn_slice(slots[j % 2], l, s)
            else:
                load_ple(slots[j % 2], l)

        def hv(c, c0, n):
            return Hreg.v(c, (c0, c0 + n))

        def hv_tile(c0, n):
            return V(Hreg.ap[:, :, c0:c0 + n], bufs_of([hv(c, c0, n) for c in range(KC)]))

        def hkey(c0):
            k = "H%d" % c0
            if k not in KEY:
                KEY[k] = Buf("key_" + k)
            return KEY[k]

        def rmsnorm(c0, n, gname, outs, rs_pool, final=False):
            ps = psum(n)
            sqs, sd = rs_pool
            for c in range(KC):
                sq = sqs[c % len(sqs)].v((0, n))
                act(sq, hv(c, c0, n), AF.Square)
                mm(ps, onesD.all(), sq, c == 0, c == KC - 1)
            sdv = sd.v((0, n))
            act(sdv, ps, AF.Sqrt, bias=epsT.all())
            emit("dve", lambda e: e.reciprocal(out=sdv.ap, in_=sdv.ap), R=[sdv], W=[sdv])
            for c in range(KC):
                stt(outs[c], hv(c, c0, n), vcol(gname, c), sdv, ALU.mult, ALU.mult)

        first_group = True
        for gi, G in enumerate(GROUPS):
            npz = G["np"]
            ng = npz + (NS if G["sample"] else 0)
            mix_tiles = [(i * 256, 256, "p") for i in range(npz // 256)]
            ffn_tiles = [(i * 512, 512, "p") for i in range(npz // 512)]
            if G["sample"]:
                mix_tiles.append((npz, NS, "s"))
                ffn_tiles.append((npz, NS, "s"))
            last_prompt_group = (G["p0"] + npz == SEQ)

            xv = xT.rearrange("(c p) n -> p c n", p=128)
            for (c0, n, kind) in mix_tiles:
                gcol = G["p0"] + c0 if kind == "p" else SEQ
                if gi == 0 and c0 == 0:
                    for (eq, ca, cb) in (("sp", 0, 4), ("act", 4, 8)):
                        hvh = V(Hreg.ap[:, ca:cb, c0:c0 + n], bufs_of([hv(c, c0, n) for c in range(ca, cb)]))
                        dma(eq, hvh, xv[:, ca:cb, gcol:gcol + n], key=hkey(c0))
                    continue
                hvw = hv_tile(c0, n)
                dma("sp", hvw, xv[:, :, gcol:gcol + n], key=hkey(c0))
            if G["sample"]:
                dma("sp", h0s.all(), st_h0.rearrange("(c p) n -> p c n", p=128), h0s.all())
                dma("sp", lcs.all(), st_lc.rearrange("(c p) (s k) -> p c s k", p=128, k=3), lcs.all())
            if first_group:
                load_mixer0()
            startup_blocks = first_group
            first_group = False

            def load_P(l):
                pv = pT.rearrange("l (c p) n -> l p c n", p=128)
                for k in range(2):
                    dv = Preg.v(k, (0, npz))
                    dma("pool", dv, pv[l, :, k, G["p0"]:G["p0"] + npz], key=KEY["P"])
                    if G["sample"]:
                        dv2 = Preg.v(k, (npz, npz + NS))
                        dma("pool", dv2, pv[l, :, k, SEQ:SEQ + NS], key=KEY["P"])

            for l in range(2):
                arena_reset()
                sqs = [T(mem, [256], BF16) for _ in range(3)]
                sd = T(mem, [256], F32)
                xns = [T(mem, [KC, 256], BF16) for _ in range(2)]
                if l == 0:
                    diagA = T(mem, [4, 4, 128], BF16)
                    diagB = T(mem, [4, 4, 128], BF16, off=Preg.off)

                    def diag_v(j, c):
                        return (diagA if c < 4 else diagB).v(j, c % 4)
                    for j in range(4):
                        for c in range(KC):
                            ts(diag_v(j, c), ident.all(), vcol("lcw", j * 8 + c), ALU.mult, 0.0, ALU.add, eng=POOLENG)
                    gg = [T(mem, [256], BF16) for _ in range(8)]
                    xbuf = [T(mem, [260], BF16, align=64) for _ in range(3)]
                    xbuf_s = [T(mem, [NSEQ, 7], BF16, align=64) for _ in range(2)]
                    xc = [T(mem, [256], BF16) for _ in range(8)]
                    f32s = [T(mem, [256], F32) for _ in range(16)] + [T(mem, [256], F32, off=OFC.off + 1024 * q) for q in range(3)] + [T(mem, [256], F32, off=fcs.off + 1024 * q) for q in range(3)]
                    thr_b = f32s[0:2]
                    av_b = f32s[2:10]
                    a2m_b = f32s[10:14]
                    thi_b = f32s[14:22]
                    t16 = T(mem, [NSEQ], F32, align=64)
                    gated = [T(mem, [KC, 256], BF16) for _ in range(1)]
                    print("LRU arena top", mem.top)
                    xcnt = [0]
                    ntl = len(mix_tiles)

                    def tinfo(ti):
                        c0, n, kind = mix_tiles[ti]
                        smp = kind == "s"
                        last_p = (kind == "p") and last_prompt_group and (c0 + n == npz)
                        return c0, n, smp, last_p

                    def prologue_a(ti):
                        c0, n, smp, last_p = tinfo(ti)
                        ps = psum(n)
                        for c in range(KC):
                            sq = sqs[c % 3].v((0, n))
                            tt(sq, hv(c, c0, n), hv(c, c0, n), ALU.mult, eng=POOLENG)
                            mm(ps, onesD.all(), sq, c == 0, c == KC - 1)
                        return ps

                    def prologue_b(ti, ps):
                        c0, n, smp, last_p = tinfo(ti)
                        xn = xns[ti % 2]
                        sdv = sd.v((0, n))
                        act(sdv, ps, AF.Sqrt, bias=epsT.all())
                        emit("dve", lambda e: e.reciprocal(out=sdv.ap, in_=sdv.ap), R=[sdv], W=[sdv])
                        for c in range(KC):
                            stt(xn.v(c, (0, n)), hv(c, c0, n), vcol("nm0", c), sdv, ALU.mult, ALU.mult)

                    def Xp(ti, half):
                        c0, n, smp, last_p = tinfo(ti)
                        xn = xns[ti % 2]
                        pend = []

                        def xbp(c):
                            i = xcnt[0]
                            xcnt[0] += 1
                            ps = psum(n)
                            for k in range(KC):
                                mm(ps, R1a.v(k, (1024 + c * 128, 1024 + (c + 1) * 128)), xn.v(k, (0, n)), k == 0, k == KC - 1)
                            if not smp:
                                xb = xbuf[i % 3]
                                cp(xb.v((0, 3)), xh_store.v(c, (0, 3)), eng=POOLENG)
                                cp(xb.v((3, 3 + n)), ps, eng="act")
                                cp(xh_store.v(c, (0, 3)), xb.v((n, n + 3)), eng=POOLENG)
                                if last_p:
                                    cp(OLC.v(c, 0), V(ps.ap[:, n - 3:n], ps.bufs), eng="act")
                            else:
                                xb = xbuf_s[i % 2]
                                cp(xb.v(None, (0, 3)), lcs.v(c), eng=POOLENG)
                                ps3 = V(ps.ap.rearrange("p (s t) -> p s t", t=4), ps.bufs)
                                cp(xb.v(None, (3, 7)), ps3, eng="act")
                                cp(OLC.v(c, (1, 17)), V(ps3.ap[:, :, 1:4], ps.bufs), eng="act")
                            return xb

                        def convp(c, xb):
                            ps2 = psum(n)
                            if not smp:
                                for j in range(4):
                                    mm(ps2, diag_v(j, c), xb.v((j, j + n)), j == 0, j == 3)
                            else:
                                ps23 = V(ps2.ap.rearrange("p (s t) -> p s t", t=4), ps2.bufs)
                                for j in range(4):
                                    mm(ps23, diag_v(j, c), xb.v(None, (j, j + 4)), j == 0, j == 3)
                            act(xc[c].v((0, n)), ps2, AF.Identity, bias=vcol("lcb", c))

                        cs = list(range(4 * half, 4 * half + 4))
                        xb0 = xbp(cs[0])
                        xb1 = xbp(cs[1])
                        convp(cs[0], xb0)
                        xb2 = xbp(cs[2])
                        convp(cs[1], xb1)
                        xb3 = xbp(cs[3])
                        convp(cs[2], xb2)
                        convp(cs[3], xb3)

                    def Rp(ti, half):
                        c0, n, smp, last_p = tinfo(ti)
                        for h in (2 * half, 2 * half + 1):
                            for c in (2 * h, 2 * h + 1):
                                cl = c % 2
                                psr = psum(n)
                                for kl in range(2):
                                    mm(psr, R1b_wa.v(2 * h + kl, (cl * 128, (cl + 1) * 128)), xc[2 * h + kl].v((0, n)), kl == 0, kl == 1)
                                psi = psum(n)
                                for kl in range(2):
                                    mm(psi, R1b_wx.v(2 * h + kl, (cl * 128, (cl + 1) * 128)), xc[2 * h + kl].v((0, n)), kl == 0, kl == 1)
                                thr = thr_b[c % 2].v((0, n))
                                act(thr, psr, AF.Tanh, bias=dcol(2, c), scale=0.5)
                                act(thi_b[c].v((0, n)), psi, AF.Tanh, bias=dcol(3, c), scale=0.5)
                                act(av_b[c].v((0, n)), thr, AF.Exp, bias=dcol(0, c), scale=dcol(0, c))

                    def Gp(ti, half):
                        c0, n, smp, last_p = tinfo(ti)
                        xn = xns[ti % 2]
                        for c in range(4 * half, 4 * half + 4):
                            psg = psum(n)
                            for k in range(KC):
                                mm(psg, R1a.v(k, (c * 128, (c + 1) * 128)), xn.v(k, (0, n)), k == 0, k == KC - 1)
                            act(gg[c].v((0, n)), psg, AF.Gelu_apprx_tanh)

                    def ESp(ti, half):
                        c0, n, smp, last_p = tinfo(ti)
                        for c in range(4 * half, 4 * half + 4):
                            av = av_b[c].v((0, n))
                            tt(a2m_b[c % 4].v((0, n)), av, av, ALU.mult, eng=POOLENG)
                            ts(a2m_b[c % 4].v((0, n)), a2m_b[c % 4].v((0, n)), 1.0, ALU.min, 0.0, ALU.max, eng=POOLENG)
                        for c in range(4 * half, 4 * half + 4):
                            a2m = a2m_b[c % 4].v((0, n))
                            act(a2m, a2m, AF.Sqrt, bias=0.25, scale=-0.25)

                    def chain(ti, half):
                        c0, n, smp, last_p = tinfo(ti)
                        gt = gated[0]
                        for c in range(4 * half, 4 * half + 4):
                            hs = a2m_b[c % 4].v((0, n))
                            av = av_b[c].v((0, n))
                            a2m = a2m_b[c % 4].v((0, n))
                            thi = thi_b[c].v((0, n))
                            stt(thi, thi, 1.0, xc[c].v((0, n)), ALU.add, ALU.mult)
                            tt(thi, thi, a2m, ALU.mult)
                            if not smp:
                                emit("dve", lambda e, hs=hs, av=av, thi=thi, c=c: e.tensor_tensor_scan(
                                    out=hs.ap, data0=av.ap, data1=thi.ap, initial=carry.v((c, c + 1)).ap,
                                    op0=ALU.mult, op1=ALU.add), R=[av, thi, carry.v((c, c + 1))], W=[hs])
                                cp(carry.v((c, c + 1)), V(hs.ap[:, n - 1:n], hs.bufs))
                                if last_p:
                                    cp(OH.v(c, (0, 1)), V(hs.ap[:, n - 1:n], hs.bufs))
                            else:
                                a3 = V(av.ap.rearrange("p (s t) -> p s t", t=4)[:, :, 0], av.bufs)
                                b3 = V(thi.ap.rearrange("p (s t) -> p s t", t=4)[:, :, 0], thi.bufs)
                                tt(t16.all(), a3, h0s.v(c), ALU.mult)
                                tt(b3, b3, t16.all(), ALU.add)
                                mset(a3, 0.0)
                                emit("dve", lambda e, hs=hs, av=av, thi=thi: e.tensor_tensor_scan(
                                    out=hs.ap, data0=av.ap, data1=thi.ap, initial=0.0,
                                    op0=ALU.mult, op1=ALU.add), R=[av, thi], W=[hs])
                                cp(OH.v(c, (1, 17)), V(hs.ap.rearrange("p (s t) -> p s t", t=4)[:, :, 3], hs.bufs))
                            tt(gt.v(c, (0, n)), gg[c].v((0, n)), hs, ALU.mult)

                    def Wp(ti):
                        c0, n, smp, last_p = tinfo(ti)
                        gt = gated[0]
                        for m in range(KC):
                            ps = psum(n)
                            for k in range(KC):
                                mm(ps, R1b_wout.v(k, (m * 128, (m + 1) * 128)), gt.v(k, (0, n)), k == 0, k == KC - 1)
                            tt(hv(m, c0, n), hv(m, c0, n), ps, ALU.add)

                    pps = prologue_a(0)
                    if startup_blocks:
                        load_mixer0_rest_a()
                    prologue_b(0, pps)
                    halves = [(ti, hf) for ti in range(ntl) for hf in range(2)]
                    for i, (ti, hf) in enumerate(halves):
                        Xp(ti, hf)
                        if startup_blocks and i == 0:
                            load_mixer0_rest_b()
                        Rp(ti, hf)
                        nxt_ps = None
                        if hf == 0 and ti + 1 < ntl:
                            nxt_ps = prologue_a(ti + 1)
                        if hf == 1 and ti > 0:
                            Wp(ti - 1)
                        if i > 0:
                            ESp(*halves[i - 1])
                            chain(*halves[i - 1])
                        if nxt_ps is not None:
                            prologue_b(ti + 1, nxt_ps)
                        Gp(ti, hf)
                        if startup_blocks and i == 1:
                            issue_next_block()
                            issue_next_block()
                    ESp(*halves[-1])
                    chain(*halves[-1])
                    Wp(ntl - 1)
                    if last_prompt_group:
                        dma("sp", o_h.rearrange("(c p) n -> p c n", p=128), OH.all(), OH.all())
                        dma("sp", o_lc.rearrange("(c p) (s k) -> p c s k", p=128, k=3), OLC.all(), OLC.all())
                        final_bufs.extend(OH.bufs + OLC.bufs)
                    load_mixer1_in()
                else:
                    load_mixer1_out()
                    lng = T(mem, [D], F32)
                    lnb = T(mem, [D], F32)
                    dma("sp", lng.all(), lng_d, lng.all())
                    dma("sp", lnb.all(), lnb_d, lnb.all())
                    wsTm = T(mem, [KC, 128], BF16)
                    bs_hi = T(mem, [1024], BF16, parts=1)
                    bs_lo = T(mem, [1024], BF16, parts=1)
                    bd = T(mem, [KC, 64], BF16, parts=64)
                    tmp_mark = mem.top
                    wsf = T(mem, [KC, 128], F32)
                    dma("sp", wsf.all(), wsT_d.rearrange("p (g t) -> p g t", g=8), wsf.all())
                    for g in range(8):
                        tt(wsTm.v(g), wsf.v(g), cst_f.v((128, 256)), ALU.mult)
                    bsf = T(mem, [1024], F32, parts=1)
                    dma("sp", bsf.all(), bsrow, bsf.all())
                    bs_t = T(mem, [1024], F32, parts=1)
                    cp(bs_hi.all(), bsf.all())
                    cp(bs_t.all(), bs_hi.all())
                    tt(bs_t.all(), bsf.all(), bs_t.all(), ALU.subtract)
                    cp(bs_lo.all(), bs_t.all())
                    if G["sample"]:
                        bdf = T(mem, [KC, 64], F32, parts=64)
                        dma("sp", bdf.all(), bdraw.rearrange("p (g t) -> p g t", g=8), bdf.all())
                        m64t = T(mem, [64], F32, parts=64)
                        dma("sp", m64t.all(), m64, m64t.all())
                        for g in range(8):
                            tt(bd.v(g), bdf.v(g), m64t.all(), ALU.mult)
                    mem.top = tmp_mark
                    dump2("wsTm", V(wsTm.ap.rearrange("p g t -> p (g t)"), wsTm.bufs), [128, 1024], BF16, gi)
                    dump2("bshi", bs_hi.all(), [1, 1024], BF16, gi)
                    dump2("bslo", bs_lo.all(), [1, 1024], BF16, gi)
                    vbuf = [T(mem, [D], F32) for _ in range(2)]
                    vT = [T(mem, [D], BF16) for _ in range(4)]
                    stats = LNS
                    mv = LNM
                    ubuf = [T(mem, [256], F32) for _ in range(2)]
                    gated = [T(mem, [KC, 256], BF16) for _ in range(1)]
                    print("gMLP arena top", mem.top, "stats off", stats.off, "mv off", mv.off, "vbuf", [v.off for v in vbuf], "vT", [v.off for v in vT])
                    qcnt = [0]
                    ntl = len(mix_tiles)
                    vts_of = {}

                    def g_pro(ti):
                        c0, n, kind = mix_tiles[ti]
                        xn = xns[ti % 2]
                        rmsnorm(c0, n, "nm1", [xn.v(c, (0, n)) for c in range(KC)], (sqs, sd))

                    def g_V(ti):
                        c0, n, kind = mix_tiles[ti]
                        smp = kind == "s"
                        xn = xns[ti % 2]
                        nq = 1 if smp else n // 128
                        nt = NS if smp else 128
                        vts = []
                        for q in range(nq):
                            qi = qcnt[0]
                            qcnt[0] += 1
                            pv2 = psum2()
                            for hh in range(2):
                                pvv = V(pv2[hh].ap[0:nt, :], pv2[hh].bufs)
                                for k in range(KC):
                                    mm(pvv, xn.v(k, (q * 128, q * 128 + nt)), R1a.v(k, (1024 + hh * 512, 1024 + (hh + 1) * 512)), k == 0, k == KC - 1)
                            vb = vbuf[qi % 2]
                            for hh in range(2):
                                act(vb.v((hh * 512, (hh + 1) * 512), p=(0, nt)), V(pv2[hh].ap[0:nt, :], pv2[hh].bufs), AF.Gelu_apprx_tanh)
                            for hh in range(2):
                                emit("dve", lambda e, vb=vb, hh=hh, nt=nt: e.bn_stats(out=stats.v((hh * 6, hh * 6 + 6), p=(0, nt)).ap, in_=vb.v((hh * 512, (hh + 1) * 512), p=(0, nt)).ap),
                                     R=[vb.all()], W=[stats.all()])
                            emit("dve", lambda e, nt=nt: e.bn_aggr(out=mv.v((0, 2), p=(0, nt)).ap, in_=stats.v(p=(0, nt)).ap), R=[stats.all()], W=[mv.all()])
                            act(mv.v((2, 3), p=(0, nt)), mv.v((1, 2), p=(0, nt)), AF.Sqrt, bias=epsT.v(p=(0, nt)))
                            emit("dve", lambda e, nt=nt: e.reciprocal(out=mv.v((3, 4), p=(0, nt)).ap, in_=mv.v((2, 3), p=(0, nt)).ap), R=[mv.all()], W=[mv.all()])
                            vbv = vb.v(p=(0, nt))
                            ts(vbv, vbv, mv.v((0, 1), p=(0, nt)), ALU.subtract, mv.v((3, 4), p=(0, nt)), ALU.mult)
                            tt(vbv, vbv, lng.v(p=(0, nt)), ALU.mult, eng="pool")
                            vt = vT[qi % 4]
                            if smp:
                                tt(vbv, vbv, lnb.v(p=(0, nt)), ALU.add, eng="pool")
                                cp(vt.v(p=(0, nt)), vbv, eng="act")
                                dma("sp", o_v, vbv, vbv)
                                final_bufs.extend(vb.bufs)
                            else:
                                tt(vt.v(p=(0, nt)), vbv, lnb.v(p=(0, nt)), ALU.add, eng="pool")
                            vts.append(vt)
                        vts_of[ti] = vts

                    def g_S(ti):
                        c0, n, kind = mix_tiles[ti]
                        smp = kind == "s"
                        xn = xns[ti % 2]
                        gt = gated[0]
                        vts = vts_of[ti]
                        nq = 1 if smp else n // 128
                        for g in range(8):
                            psu = psum(n)
                            for k in range(KC):
                                mm(psu, R1a.v(k, (g * 128, (g + 1) * 128)), xn.v(k, (0, n)), k == 0, k == KC - 1)
                            uv = ubuf[g % 2].v((0, n))
                            act(uv, psu, AF.Gelu_apprx_tanh)
                            pss = psum(n)
                            if not smp:
                                for q in range(nq):
                                    sub = V(pss.ap[:, q * 128:(q + 1) * 128], pss.bufs)
                                    mm(sub, ones1.v(p=(0, 1)), bs_hi.v((g * 128, (g + 1) * 128)), q == 0, False)
                                    mm(sub, ones1.v(p=(0, 1)), bs_lo.v((g * 128, (g + 1) * 128)), False, False)
                                    mm(sub, vts[q].v((g * 128, (g + 1) * 128)), wsTm.v(g), False, q == nq - 1)
                            else:
                                hi4 = V(bs_hi.ap[0:1, g * 128:g * 128 + 4].unsqueeze(1).broadcast_to([1, NSEQ, 4]), bs_hi.bufs)
                                lo4 = V(bs_lo.ap[0:1, g * 128:g * 128 + 4].unsqueeze(1).broadcast_to([1, NSEQ, 4]), bs_lo.bufs)
                                ps3 = V(pss.ap.rearrange("p (s t) -> p s t", t=4), pss.bufs)
                                mm(ps3, ones1.v(p=(0, 1)), hi4, True, False)
                                mm(ps3, ones1.v(p=(0, 1)), lo4, False, False)
                                mm(pss, vts[0].v((g * 128, (g + 1) * 128), p=(0, NS)), bd.v(g), False, True)
                            tt(gt.v(g, (0, n)), uv, pss, ALU.mult)

                    def g_W(ti):
                        c0, n, kind = mix_tiles[ti]
                        gt = gated[0]
                        for m in range(KC):
                            ps = psum(n)
                            for k in range(KC):
                                mm(ps, R1b_wout.v(k, (m * 128, (m + 1) * 128)), gt.v(k, (0, n)), k == 0, k == KC - 1)
                            tt(hv(m, c0, n), hv(m, c0, n), ps, ALU.add)

                    g_pro(0)
                    g_V(0)
                    for ti in range(ntl):
                        if ti + 1 < ntl:
                            g_pro(ti + 1)
                            g_V(ti + 1)
                        g_S(ti)
                        g_W(ti)
                    if gi + 1 < len(GROUPS):
                        dma("pool", R1a.v(None, (1024, 2048)), w_lin[:, 1024:2048].rearrange("(k p) n -> p k n", p=128), key=KEY["R1a"])
                        dma("pool", R1a.v(None, (0, 1024)), w_lin[:, 0:1024].rearrange("(k p) n -> p k n", p=128), key=KEY["R1ag"])

                dump(l * 3 + 0, DBG_GROUP, gi, ng)
                if stop_after == ("mix", gi, l):
                    break
                arena_reset()
                sqs = [T(mem, [512], BF16) for _ in range(3)]
                sd = T(mem, [512], F32)
                gbuf = [T(mem, [2 + npz], BF16) for _ in range(4)]
                gbuf_s = [T(mem, [NSEQ, 6], BF16) for _ in range(4)]
                ggb = [T(mem, [512], F32) for _ in range(2)]
                zb = [T(mem, [4, 512], BF16) for _ in range(2)]
                dg = [T(mem, [3, 4, 128], BF16) for _ in range(2)]
                print("FFN arena top", mem.top)
                if G["sample"]:
                    dma("sp", fcs.all(), st_fc[l].rearrange("(c p) (s k) -> p c s k", p=128, k=2), fcs.all())
                for (c0, n, kind) in ffn_tiles:
                    rmsnorm(c0, n, "nf%d" % l, [XNall.v(c, (c0, c0 + n)) for c in range(KC)], (sqs, sd))
                load_P(l)
                def ffn_prep(s):
                    dgs = dg[s % 2]
                    for j in range(3):
                        for cl in range(4):
                            ts(dgs.v(j, cl), ident.all(), vcol("fcw%d" % l, j * 24 + s * 4 + cl), ALU.mult)
                    for cl in range(4):
                        cp(gbuf[cl].v((0, 2)), fh_store.v(l, s * 4 + cl))

                def ffn_U(s, ti, z):
                    bj = (gi * 2 + l) * (NSL + 1) + s
                    slot = slots[bj % 2]
                    dgs = dg[s % 2]
                    c0, n, kind = ffn_tiles[ti]
                    smp = kind == "s"
                    last_p = (kind == "p") and last_prompt_group and (c0 + n == npz)
                    psus = {}

                    def Gp(cl):
                        cg = s * 4 + cl
                        psg = psum(n)
                        for k in range(KC):
                            mm(psg, slot.wg.v(k, (cl * 128, (cl + 1) * 128)), XNall.v(k, (c0, c0 + n)), k == 0, k == KC - 1)
                        if not smp:
                            cp(gbuf[cl].v((2 + c0, 2 + c0 + n)), psg, eng="act")
                            if last_p:
                                cp(OFC.v(cg, 0), V(psg.ap[:, n - 2:n], psg.bufs), eng="act")
                            if c0 + n == npz:
                                cp(fh_store.v(l, cg), gbuf[cl].v((npz, npz + 2)))
                        else:
                            gs = gbuf_s[cl]
                            cp(gs.v(None, (0, 2)), fcs.v(cg))
                            psg3 = V(psg.ap.rearrange("p (s t) -> p s t", t=4), psg.bufs)
                            cp(gs.v(None, (2, 6)), psg3, eng="act")
                            cp(OFC.v(cg, (1, 17)), V(psg3.ap[:, :, 2:4], psg.bufs), eng="act")

                    def Up(cl):
                        psu = psum(n)
                        for k in range(KC):
                            mm(psu, slot.wu.v(k, (cl * 128, (cl + 1) * 128)), XNall.v(k, (c0, c0 + n)), k == 0, k == KC - 1)
                        psus[cl] = psu

                    def Cp(cl):
                        cg = s * 4 + cl
                        psc = psum(n)
                        if not smp:
                            for j in range(3):
                                mm(psc, dgs.v(j, cl), gbuf[cl].v((c0 + j, c0 + j + n)), j == 0, j == 2)
                        else:
                            gs = gbuf_s[cl]
                            psc3 = V(psc.ap.rearrange("p (s t) -> p s t", t=4), psc.bufs)
                            for j in range(3):
                                mm(psc3, dgs.v(j, cl), gs.v(None, (j, j + 4)), j == 0, j == 2)
                        ggv = ggb[cl % 2].v((0, n))
                        act(ggv, psc, AF.Gelu_apprx_tanh, bias=vcol("fcb%d" % l, cg))
                        tt(z.v(cl, (0, n)), ggv, psus[cl], ALU.mult)

                    Gp(0); Up(0); Gp(1); Cp(0); Up(1); Gp(2); Cp(1); Up(2); Gp(3); Cp(2); Up(3); Cp(3)

                def ffn_D(s, ti, z):
                    bj = (gi * 2 + l) * (NSL + 1) + s
                    slot = slots[bj % 2]
                    c0, n, kind = ffn_tiles[ti]
                    for m in range(KC):
                        ps = psum(n)
                        for k in range(4):
                            mm(ps, slot.wd.v(k, (m * 128, (m + 1) * 128)), z.v(k, (0, n)), k == 0, k == 3)
                        tt(hv(m, c0, n), hv(m, c0, n), ps, ALU.add)

                items = [(s, ti) for s in range(NSL) for ti in range(len(ffn_tiles))]
                prev = None
                for idx, (s, ti) in enumerate(items):
                    if ti == 0:
                        ffn_prep(s)
                    z = zb[idx % 2]
                    ffn_U(s, ti, z)
                    if prev is not None:
                        ffn_D(*prev)
                        if prev[1] == len(ffn_tiles) - 1:
                            issue_next_block()
                    prev = (s, ti, z)
                ffn_D(*prev)
                issue_next_block()
                if last_prompt_group:
                    dma("sp", o_fc[l].rearrange("(c p) (s k) -> p c s k", p=128, k=2), OFC.all(), OFC.all())
                    final_bufs.extend(OFC.bufs)
                if l == 1 and gi + 1 < len(GROUPS):
                    load_mat(R1b_wa, w_la, "R1b")
                    load_mat(R1b_wx, w_lx, "R1b")
                    load_mat(R1b_wout, w_lout, "R1bo")

                dump(l * 3 + 1, DBG_GROUP, gi, ng)
                if stop_after == ("ffn", gi, l):
                    break
                arena_reset()
                sqs = [T(mem, [512], BF16) for _ in range(3)]
                sd = T(mem, [512], F32)
                sqs2 = [T(mem, [512], BF16) for _ in range(2)]
                sd2 = T(mem, [512], F32)
                xns = [T(mem, [KC, 512], BF16) for _ in range(2)]
                thb = [T(mem, [512], F32) for _ in range(2)]
                tb = [T(mem, [512], F32) for _ in range(2)]
                bj = (gi * 2 + l) * (NSL + 1) + NSL
                slot = slots[bj % 2]
                ptiles = ffn_tiles
                yv = yT.rearrange("(c p) n -> p c n", p=128)

                def p_pro(ti):
                    c0, n, kind = ptiles[ti]
                    xn = xns[ti % 2]
                    rmsnorm(c0, n, "np%d" % l, [xn.v(c, (0, n)) for c in range(KC)], (sqs, sd))

                def p_body(ti, ms):
                    c0, n, kind = ptiles[ti]
                    xn = xns[ti % 2]
                    for m in ms:
                        psg = psum(n)
                        for k in range(KC):
                            mm(psg, slot.pgate.v(k, (m * 128, (m + 1) * 128)), xn.v(k, (0, n)), k == 0, k == KC - 1)
                        th = thb[m % 2].v((0, n))
                        act(th, psg, AF.Sigmoid)
                        psp = psum(n)
                        for k in range(2):
                            mm(psp, slot.pproj.v(k, (m * 128, (m + 1) * 128)), Preg.v(k, (c0, c0 + n)), k == 0, k == 1)
                        tv = tb[m % 2].v((0, n))
                        tt(tv, th, psp, ALU.mult)
                        tt(hv(m, c0, n), hv(m, c0, n), tv, ALU.add, eng="pool")

                def p_final(ti):
                    c0, n, kind = ptiles[ti]
                    rmsnorm(c0, n, "nfin", [hv(c, c0, n) for c in range(KC)], (sqs2, sd2))
                    gcol = G["p0"] + c0 if kind == "p" else SEQ
                    hvr = hv_tile(c0, n)
                    dma("sp", yv[:, :, gcol:gcol + n], hvr, key=hkey(c0))
                    final_bufs.append(hkey(c0))

                p_pro(0)
                for ti in range(len(ptiles)):
                    p_body(ti, range(0, 2))
                    if ti + 1 < len(ptiles):
                        p_pro(ti + 1)
                    p_body(ti, range(2, KC))
                    if l == 1 and ti > 0:
                        p_final(ti - 1)
                if l == 1:
                    p_final(len(ptiles) - 1)
                issue_next_block()
                dump(l * 3 + 2, DBG_GROUP, gi, ng)
            else:
                continue
            break
        P.finalize(final_wait_bufs=final_bufs)
        nops = {e: len(P.ops[e]) for e in ENGS}
        print("ops per engine", nops)
    return nc


_NC_CACHE = {}
_DBG = {}


def _chunks(v, n):
    return np.ascontiguousarray(np.asarray(v, np.float32).reshape(n, 128).T)


def kernel(x_prompt, x_sample, p_prompt, p_sample, state_lru_h, state_lru_conv, state_ffn_conv,
           norm_mix, norm_ffn, norm_ple, norm_final,
           lru_w_in, lru_conv_w, lru_conv_b, lru_w_a, lru_b_a, lru_w_x, lru_b_x, lru_lambda, lru_w_out,
           gm_w_in, gm_ln_g, gm_ln_b, gm_w_s, gm_b_s, gm_w_out,
           ffn_w_up, ffn_conv_w, ffn_conv_b, ffn_w_down,
           ple_w_gate, ple_w_proj, _stop_after=None, _debug=False):
    f = lambda a: np.asarray(a, np.float32)
    x_prompt, x_sample, p_prompt, p_sample = f(x_prompt), f(x_sample), f(p_prompt), f(p_sample)
    state_lru_h, state_lru_conv, state_ffn_conv = f(state_lru_h), f(state_lru_conv), f(state_ffn_conv)
    n_cores = 8
    cols = [_chunks(f(norm_mix)[0], 8), _chunks(f(norm_mix)[1], 8), _chunks(f(norm_ffn)[0], 8), _chunks(f(norm_ffn)[1], 8),
            _chunks(f(norm_ple)[0], 8), _chunks(f(norm_ple)[1], 8), _chunks(f(norm_final), 8)]
    cols += [_chunks(f(lru_conv_w)[0, j], 8) for j in range(4)]
    cols += [_chunks(f(lru_conv_b)[0], 8), _chunks(f(lru_b_a)[0], 8), _chunks(f(lru_b_x)[0], 8), _chunks(f(lru_lambda)[0], 8)]
    for l in range(2):
        cols += [_chunks(f(ffn_conv_w)[l, j], 24) for j in range(3)]
    cols += [_chunks(f(ffn_conv_b)[0], 24), _chunks(f(ffn_conv_b)[1], 24)]
    vecs = np.ascontiguousarray(np.concatenate(cols, axis=1))
    assert vecs.shape == (128, NV), vecs.shape
    ident = np.eye(128, dtype=np.float32)
    mask128 = np.triu(np.ones((128, 128), np.float32))
    cst = np.ascontiguousarray(np.concatenate([ident, mask128], axis=1))
    m64 = np.zeros((64, 64), np.float32)
    for q in range(16):
        m64[q * 4:(q + 1) * 4, q * 4:(q + 1) * 4] = np.triu(np.ones((4, 4), np.float32))
    ws = f(gm_w_s)[0]
    wsT = np.ascontiguousarray(ws.transpose(2, 0, 1).reshape(128, 1024))
    blk = ws[:, 0:4, 0:4].transpose(2, 0, 1)
    bdraw = np.ascontiguousarray(np.tile(blk[None, :, :, None, :], (16, 1, 1, 16, 1)).reshape(64, 8 * 64))
    bsrow = np.ascontiguousarray(f(gm_b_s)[0].reshape(1, 1024))
    lng_bc = np.ascontiguousarray(np.tile(f(gm_ln_g)[0][None, :], (128, 1)))
    lnb_bc = np.ascontiguousarray(np.tile(f(gm_ln_b)[0][None, :], (128, 1)))
    shared = {
        "vecs": vecs, "cst": cst, "m64": m64, "bdraw": bdraw, "bsrow": bsrow,
        "lng_bc": lng_bc, "lnb_bc": lnb_bc, "wsT": wsT,
        "lru_w_in": np.ascontiguousarray(f(lru_w_in)[0]),
        "lru_w_a": np.ascontiguousarray(f(lru_w_a)[0].reshape(1024, 256)),
        "lru_w_x": np.ascontiguousarray(f(lru_w_x)[0].reshape(1024, 256)),
        "lru_w_out": np.ascontiguousarray(f(lru_w_out)[0]),
        "gm_w_in": np.ascontiguousarray(f(gm_w_in)[0]),
        "gm_w_out": np.ascontiguousarray(f(gm_w_out)[0]),
        "ffn_w_up": np.ascontiguousarray(f(ffn_w_up)),
        "ffn_w_down": np.ascontiguousarray(f(ffn_w_down)),
        "ple_w_gate": np.ascontiguousarray(f(ple_w_gate)),
        "ple_w_proj": np.ascontiguousarray(f(ple_w_proj)),
    }
    in_maps = []
    for i in range(n_cores):
        sq = slice(16 * i, 16 * i + 16)
        xT = np.concatenate([x_prompt[i].T, x_sample[sq].reshape(64, 1024).T], axis=1)
        pT = np.stack([np.concatenate([p_prompt[l, i].T, p_sample[l, sq].reshape(64, 256).T], axis=1) for l in range(2)])
        m = dict(shared)
        m["xT"] = np.ascontiguousarray(xT)
        m["pT"] = np.ascontiguousarray(pT)
        m["st_h0"] = np.ascontiguousarray(state_lru_h[0, sq].T)
        m["st_lc"] = np.ascontiguousarray(state_lru_conv[0, sq].transpose(2, 0, 1).reshape(1024, 48))
        m["st_fc"] = np.ascontiguousarray(state_ffn_conv[:, sq].transpose(0, 3, 1, 2).reshape(2, 3072, 32))
        in_maps.append(m)
    key = repr((_stop_after, _debug))
    if key not in _NC_CACHE:
        _NC_CACHE[key] = build_program(_stop_after, _debug)
    nc = _NC_CACHE[key]
    res = run_bass_kernel_spmd(nc, in_maps, core_ids=list(range(n_cores)))
    R = res.results
    if _debug:
        _DBG.clear()
        _DBG["dbg"] = np.stack([R[i]["dbg"] for i in range(n_cores)])
        _DBG.update({k: np.asarray(R[0][k]).astype(np.float32) for k in R[0] if k.startswith("dbg_")})
    y_prompt = np.stack([R[i]["yT"][:, :SEQ].T for i in range(n_cores)])
    y_sample = np.concatenate([R[i]["yT"][:, SEQ:].T.reshape(16, 4, 1024) for i in range(n_cores)])
    oh = [R[i]["o_h"] for i in range(n_cores)]
    olc = [R[i]["o_lc"].reshape(1024, 17, 3) for i in range(n_cores)]
    ofc = [R[i]["o_fc"].reshape(2, 3072, 17, 2) for i in range(n_cores)]
    new_lru_h_prompt = np.stack([oh[i][:, 0] for i in range(n_cores)])[None]
    new_lru_h_sample = np.concatenate([oh[i][:, 1:].T for i in range(n_cores)])[None]
    new_lru_conv_prompt = np.stack([olc[i][:, 0, :].T for i in range(n_cores)])[None]
    new_lru_conv_sample = np.concatenate([olc[i][:, 1:, :].transpose(1, 2, 0) for i in range(n_cores)])[None]
    new_ffn_conv_prompt = np.stack([ofc[i][:, :, 0, :].transpose(0, 2, 1) for i in range(n_cores)], axis=1)
    new_ffn_conv_sample = np.concatenate([ofc[i][:, :, 1:, :].transpose(0, 2, 3, 1) for i in range(n_cores)], axis=1)
    new_gm_v_sample = np.concatenate([R[i]["o_v"].reshape(16, 4, 1024) for i in range(n_cores)])[None]
    outs = (y_prompt, y_sample, new_lru_h_prompt, new_lru_conv_prompt, new_ffn_conv_prompt,
            new_lru_h_sample, new_lru_conv_sample, new_ffn_conv_sample, new_gm_v_sample)
    return tuple(np.ascontiguousarray(o, dtype=np.float32) for o in outs)
```
